# Optimizing a Trainium2 kernel written in Bass

```python
import math
import jax, jax.numpy as jnp
from jax import lax
import numpy as np

D_MODEL = 2048
BATCH = 1
SEQ = 8192
DEPTH = 1

MIX_WIDTH = D_MODEL
ATTN_WIDTH = MIX_WIDTH // 2
GMLP_WIDTH = MIX_WIDTH - ATTN_WIDTH
N_DIFF_HEADS = 8
DIFF_HEAD_DIM = ATTN_WIDTH // (2 * N_DIFF_HEADS)
DIFF_V_DIM = 2 * DIFF_HEAD_DIM
N_GMLP_GROUPS = 8
GMLP_GROUP_DIM = GMLP_WIDTH // N_GMLP_GROUPS
CHUNK = 128
Q_BLOCK = 128
ROPE_THETA = 10000.0
D_FF = -(-8 * D_MODEL // (3 * 256)) * 256
RMS_EPS = 1e-6
LN_EPS = 1e-5
SUBLN_EPS = 1e-5
Q_WIDTH = 2 * N_DIFF_HEADS * DIFF_HEAD_DIM
K_WIDTH = Q_WIDTH
V_WIDTH = N_DIFF_HEADS * DIFF_V_DIM
IN_PROJ_WIDTH = Q_WIDTH + K_WIDTH + V_WIDTH + 2 * GMLP_WIDTH

kernel_name = "hybrid_diffattn_gmlp_encoder_layer"


def _rmsnorm(x, g, eps=RMS_EPS):
    xf = x.astype(jnp.float32)
    y = xf * lax.rsqrt(jnp.mean(xf * xf, axis=-1, keepdims=True) + eps)
    return (y * g.astype(jnp.float32)).astype(x.dtype)


def _layernorm(x, g, b, eps=LN_EPS):
    xf = x.astype(jnp.float32)
    mu = jnp.mean(xf, axis=-1, keepdims=True)
    var = jnp.mean(jnp.square(xf - mu), axis=-1, keepdims=True)
    y = (xf - mu) * lax.rsqrt(var + eps)
    return (y * g.astype(jnp.float32) + b.astype(jnp.float32)).astype(x.dtype)


def _rope_tables(positions):
    inv_freq = ROPE_THETA ** (-jnp.arange(0, DIFF_HEAD_DIM, 2, dtype=jnp.float32) / DIFF_HEAD_DIM)
    ang = positions.astype(jnp.float32)[..., None] * inv_freq
    ang = jnp.concatenate([ang, ang], axis=-1)[:, :, None, :]
    return jnp.cos(ang), jnp.sin(ang)


def _apply_rope(x, cos, sin):
    xf = x.astype(jnp.float32)
    x1, x2 = jnp.split(xf, 2, axis=-1)
    rot = jnp.concatenate([-x2, x1], axis=-1)
    return (xf * cos + rot * sin).astype(x.dtype)


def _diff_attention(q, k, v, lam, lambda_init, subln_g):
    B, S = q.shape[0], q.shape[1]
    H, hd, dv = N_DIFF_HEADS, DIFF_HEAD_DIM, DIFF_V_DIM
    nblk = S // Q_BLOCK
    q = q.reshape(B, S, H, 2, hd).transpose(0, 2, 3, 1, 4)
    k = k.reshape(B, S, H, 2, hd).transpose(0, 2, 3, 1, 4)
    v = v.transpose(0, 2, 1, 3)
    qb = q.reshape(B, H, 2, nblk, Q_BLOCK, hd).transpose(3, 0, 1, 2, 4, 5)
    scale = hd ** -0.5

    def one_block(q_blk):
        s = jnp.einsum('bhiqd,bhikd->bhiqk', q_blk, k,
                       preferred_element_type=jnp.float32) * scale
        p = jax.nn.softmax(s, axis=-1)
        w = p[:, :, 0] - lam * p[:, :, 1]
        return jnp.einsum('bhqk,bhkd->bhqd', w.astype(v.dtype), v)

    out = lax.map(one_block, qb)
    out = out.transpose(1, 0, 3, 2, 4).reshape(B, S, H, dv)
    out = _rmsnorm(out, subln_g, SUBLN_EPS) * (1.0 - lambda_init)
    return out.reshape(B, S, H * dv)


def _spatial_gating(u, v, ln_g, ln_b, w_s, b_s):
    B, S = u.shape[0], u.shape[1]
    n = S // CHUNK
    v = _layernorm(v, ln_g, ln_b)
    vc = v.reshape(B, n, CHUNK, N_GMLP_GROUPS, GMLP_GROUP_DIM)
    y = jnp.einsum('gpq,bnqgc->bnpgc', w_s, vc) + b_s.T[None, None, :, :, None]
    return u * y.reshape(B, S, GMLP_WIDTH)


def setup_inputs(seed: int = 0) -> dict:
    key = jax.random.key(seed)
    ks = jax.random.split(key, 20)
    f32 = jnp.float32
    nrm = lambda k, shape, s: jax.random.normal(k, shape, f32) * s
    gain = lambda k, shape: 1.0 + 0.01 * jax.random.normal(k, shape, f32)
    x = jax.random.normal(ks[0], (BATCH, SEQ, D_MODEL), f32)
    offset = jax.random.randint(ks[1], (BATCH, 1), 0, 1024, dtype=jnp.int32)
    positions = jnp.arange(SEQ, dtype=jnp.int32)[None, :] + offset
    return {
        "x": x,
        "positions": positions,
        "pre_mix_g": gain(ks[2], (DEPTH, D_MODEL)),
        "w_in": nrm(ks[3], (DEPTH, D_MODEL, IN_PROJ_WIDTH), D_MODEL ** -0.5),
        "lambda_q1": nrm(ks[4], (DEPTH, DIFF_HEAD_DIM), 0.1),
        "lambda_k1": nrm(ks[5], (DEPTH, DIFF_HEAD_DIM), 0.1),
        "lambda_q2": nrm(ks[6], (DEPTH, DIFF_HEAD_DIM), 0.1),
        "lambda_k2": nrm(ks[7], (DEPTH, DIFF_HEAD_DIM), 0.1),
        "subln_g": gain(ks[8], (DEPTH, DIFF_V_DIM)),
        "gmlp_ln_g": gain(ks[9], (DEPTH, GMLP_WIDTH)),
        "gmlp_ln_b": nrm(ks[10], (DEPTH, GMLP_WIDTH), 0.01),
        "w_s": nrm(ks[11], (DEPTH, N_GMLP_GROUPS, CHUNK, CHUNK), CHUNK ** -0.5),
        "b_s": nrm(ks[12], (DEPTH, N_GMLP_GROUPS, CHUNK), 0.01),
        "w_out": nrm(ks[13], (DEPTH, MIX_WIDTH, D_MODEL), MIX_WIDTH ** -0.5),
        "post_mix_g": gain(ks[14], (DEPTH, D_MODEL)),
        "pre_ffn_g": gain(ks[15], (DEPTH, D_MODEL)),
        "w_gate": nrm(ks[16], (DEPTH, D_MODEL, D_FF), D_MODEL ** -0.5),
        "w_up": nrm(ks[17], (DEPTH, D_MODEL, D_FF), D_MODEL ** -0.5),
        "w_down": nrm(ks[18], (DEPTH, D_FF, D_MODEL), D_FF ** -0.5),
        "post_ffn_g": gain(ks[19], (DEPTH, D_MODEL)),
    }


def reference(x, positions, pre_mix_g, w_in, lambda_q1, lambda_k1, lambda_q2, lambda_k2,
              subln_g, gmlp_ln_g, gmlp_ln_b, w_s, b_s, w_out, post_mix_g,
              pre_ffn_g, w_gate, w_up, w_down, post_ffn_g):
    B, S = x.shape[0], x.shape[1]
    cos, sin = _rope_tables(positions)
    splits = [Q_WIDTH, Q_WIDTH + K_WIDTH, Q_WIDTH + K_WIDTH + V_WIDTH,
              Q_WIDTH + K_WIDTH + V_WIDTH + GMLP_WIDTH]
    for l in range(DEPTH):
        lambda_init = 0.8 - 0.6 * math.exp(-0.3 * l)
        h = _rmsnorm(x, pre_mix_g[l])
        proj = h @ w_in[l]
        q, k, v, gu, gv = jnp.split(proj, splits, axis=-1)
        q = _apply_rope(q.reshape(B, S, 2 * N_DIFF_HEADS, DIFF_HEAD_DIM), cos, sin)
        k = _apply_rope(k.reshape(B, S, 2 * N_DIFF_HEADS, DIFF_HEAD_DIM), cos, sin)
        v = v.reshape(B, S, N_DIFF_HEADS, DIFF_V_DIM)
        lam = (jnp.exp(jnp.sum(lambda_q1[l].astype(jnp.float32) * lambda_k1[l].astype(jnp.float32)))
               - jnp.exp(jnp.sum(lambda_q2[l].astype(jnp.float32) * lambda_k2[l].astype(jnp.float32)))
               + lambda_init)
        attn_out = _diff_attention(q, k, v, lam, lambda_init, subln_g[l])
        gmlp_out = _spatial_gating(jax.nn.gelu(gu, approximate=False),
                                   jax.nn.gelu(gv, approximate=False),
                                   gmlp_ln_g[l], gmlp_ln_b[l], w_s[l], b_s[l])
        mix = jnp.concatenate([attn_out, gmlp_out], axis=-1) @ w_out[l]
        x = x + _rmsnorm(mix, post_mix_g[l])
        h = _rmsnorm(x, pre_ffn_g[l])
        f = (jax.nn.silu(h @ w_gate[l]) * (h @ w_up[l])) @ w_down[l]
        x = x + _rmsnorm(f, post_ffn_g[l])
    return x
```

```python
import math
from contextlib import ExitStack
import numpy as np
import ml_dtypes
import concourse.bass as bass
import concourse.mybir as mybir
from concourse.bass_utils import run_bass_kernel_spmd

F32 = mybir.dt.float32
BF16 = mybir.dt.bfloat16
I32 = mybir.dt.int32
AF = mybir.ActivationFunctionType
ALU = mybir.AluOpType
AX = mybir.AxisListType

NCORES = 8
S = 8192
D = 2048
TOK = S // NCORES
NT = TOK // 128
KC = D // 128
DFF = 5632
INW = 5120
LAMBDA_INIT = 0.8 - 0.6 * math.exp(-0.3 * 0)
RMS_EPS = 1e-6
LN_EPS = 1e-5
SUBLN_EPS = 1e-5
PI = math.pi


class Tok:
    __slots__ = ("name", "w", "r")

    def __init__(self, name, over=()):
        self.name = name
        self.w = None
        self.r = {}
        for o in over:
            if o.w is not None:
                self.r[("w", id(o))] = o.w
            for k, v in o.r.items():
                self.r[("r", id(o), k)] = v


class Op:
    __slots__ = ("eng", "fn", "deps", "idx", "dma", "sem", "val", "inc", "need_sig", "sigval", "slot")

    def __init__(self, eng, fn, dma):
        self.eng = eng
        self.fn = fn
        self.deps = set()
        self.dma = dma
        self.sem = None
        self.val = 0
        self.inc = 16
        self.need_sig = False
        self.sigval = 0
        self.slot = None


class Prog:
    ENGS = ("pe", "act", "dve", "pool", "sp")
    NSLOT = {"sp": 8, "pool": 6}

    def __init__(self):
        self.ops = []
        self.dma_count = {"sp": 0, "pool": 0}
        self.slot_last = {}

    def add(self, eng, fn, reads=(), writes=(), dma=False, sem_override=None, relaxed=()):
        op = Op(eng, fn, dma)
        op.idx = len(self.ops)
        deps = set()
        for t in reads:
            if t.w is not None:
                deps.add(t.w)
        for t in writes:
            if t.w is not None and not (t in relaxed and t.w.eng == eng and not t.w.dma):
                deps.add(t.w)
            deps.update(t.r.values())
        if eng == "pe" and not dma:
            deps = {d for d in deps if not (d.eng == "pe" and not d.dma)}
        if dma:
            if sem_override is not None:
                op.sem, op.val, op.inc = sem_override
            else:
                k = self.dma_count[eng]
                self.dma_count[eng] = k + 1
                ns = self.NSLOT[eng]
                op.slot = (eng, k % ns)
                op.val = 16 * (k // ns + 1)
                prev = self.slot_last.get(op.slot)
                if prev is not None:
                    deps.add(prev)
                self.slot_last[op.slot] = op
        for d in deps:
            d.need_sig = True
        op.deps = deps
        key = ("dma", op.idx) if dma else eng
        for t in reads:
            t.r[key] = op
        for t in writes:
            t.w = op
            t.r = {}
        self.ops.append(op)
        return op

    def finalize(self):
        cnt = {e: 0 for e in self.ENGS}
        for op in self.ops:
            if not op.dma and op.need_sig:
                cnt[op.eng] += 1
                op.sigval = cnt[op.eng]
        return cnt

    def emit(self, eng_name, eng, engsem, dmasem):
        waited = {}
        for op in self.ops:
            if op.eng != eng_name:
                continue
            needs = {}
            for d in op.deps:
                if d.dma:
                    s = d.sem if d.sem is not None else dmasem[d.slot]
                    v = d.val
                else:
                    s = engsem[d.eng]
                    v = d.sigval
                key = s.num
                if needs.get(key, (None, 0))[1] < v:
                    needs[key] = (s, v)
            for key, (s, v) in needs.items():
                if waited.get(key, 0) < v:
                    eng.wait_ge(s, v)
                    waited[key] = v
            ins = op.fn(eng)
            if op.dma:
                s = op.sem if op.sem is not None else dmasem[op.slot]
                ins.then_inc(s, op.inc)
            elif op.need_sig:
                ins.then_inc(engsem[eng_name], 1)
        for slot, op in self.slot_last.items():
            if slot[0] == eng_name:
                s = dmasem[slot]
                if waited.get(s.num, 0) < op.val:
                    eng.wait_ge(s, op.val)
                    waited[s.num] = op.val


def toks(prefix, n):
    return [Tok(f"{prefix}{i}") for i in range(n)]


def build_nc(stop_after="E", dbg=None):
    nc = bass.Bass("TRN2", target_bir_lowering=False)
    dt = nc.dram_tensor

    def din(name, shape, dtype=F32):
        return dt(name, shape, dtype, kind="ExternalInput").ap()

    x_d = din("x", [TOK, D])
    xall_d = din("x_all", [S, D])
    pos_d = din("pos", [128, NT], I32)
    posall_d = din("pos_all", [128, S // 128], I32)
    invf_d = din("invf", [1, 32])
    ident_d = din("ident", [128, 128], BF16)
    g1c_d = din("g1c", [128, KC])
    g3c_d = din("g3c", [128, KC])
    g2_d = din("g2", [1, D])
    g4_d = din("g4", [1, D])
    lam_d = din("lamv", [1, 256])
    subg_d = din("subg", [128, 1])
    lng_d = din("lng", [1, 1024])
    lnb_d = din("lnb", [1, 1024])
    wsT_d = din("wsT", [128, 1024])
    bs_d = din("bs", [1, 1024])
    win_d = din("w_in", [D, INW])
    wout_d = din("w_out", [D, D])
    wg_d = din("w_gate", [D, DFF])
    wu_d = din("w_up", [D, DFF])
    wd_d = din("w_down", [DFF, D])
    out_d = dt("out", [TOK, D], F32, kind="ExternalOutput").ap()
    dbg_d = None
    if dbg:
        dbg_d = dt("dbg", [128, 16 * 1024], F32, kind="ExternalOutput").ap()

    k_all = dt("k_all", [1024, S], BF16)
    v_all = dt("v_all", [S, 1024], BF16)
    x1_scr = dt("x1_scr", [TOK, D], F32).ap()
    hT_scr = dt("hT_scr", [128, KC, TOK], BF16).ap()

    P = Prog()

    es = ExitStack()
    sb = lambda name, shape, dtype: es.enter_context(nc.sbuf_tensor(name, shape, dtype))
    hT = sb("hT", [128, KC, TOK], BF16)
    catT = sb("catT", [128, KC, TOK], BF16)
    big = sb("big", [128, NT, 2048], F32)
    scr = sb("scr", [128, 6, 4096], BF16)
    xb = sb("xb", [128, D], F32)
    gslot = sb("gslot", [128, D], F32)
    hb = sb("hb", [128, D], BF16)
    sm = sb("sm", [128, 512], F32)
    lamb = sb("lamb", [128, 256], F32)
    tmpa_t = sb("tmpa", [128, 256], F32)
    identb = sb("identb", [128, 128], BF16)
    onesb = sb("onesb", [128, 128], BF16)
    onesf = sb("onesf", [128, 128], F32)
    posi = sb("posi", [128, NT], I32)
    posi2 = sb("posi2", [128, 256], I32)
    posall = sb("posall", [128, S // 128], I32)
    ps = es.enter_context(nc.psum_tensor("ps", [128, 8, 512], F32))

    engsem = {e: es.enter_context(nc.semaphore("s_" + e)) for e in Prog.ENGS}
    dmasem = {}
    for e, n in Prog.NSLOT.items():
        for i in range(n):
            dmasem[(e, i)] = es.enter_context(nc.semaphore(f"d_{e}{i}"))
    block = es.enter_context(nc.Block())

    def dma(eng, out, in_, reads, writes):
        return P.add(eng, lambda e: e.dma_start(out=out, in_=in_), reads, writes, dma=True)

    def mm(out, lhsT, rhs, start, stop, reads, writes, tp=None):
        if tp is None:
            return P.add("pe", lambda e: e.matmul(out, lhsT, rhs, start=start, stop=stop), reads, writes)
        return P.add("pe", lambda e: e.matmul(out, lhsT, rhs, start=start, stop=stop, tile_position=tp),
                     reads, writes)

    t_ident = Tok("ident")
    t_ones = Tok("ones")

    def tr(out, in_, reads, writes):
        return P.add("pe", lambda e: e.transpose(out, in_, identb[:]), list(reads) + [t_ident], writes)

    def act(out, in_, func, reads, writes, scale=None, accum=None):
        kw = {}
        if scale is not None:
            kw["scale"] = scale
        if accum is not None:
            kw["accum_out"] = accum
        return P.add("act", lambda e: e.activation(out, in_, func, **kw), reads, writes)

    def dve(fn, reads, writes, eng="dve"):
        return P.add(eng, fn, reads, writes)

    def rstd_from_ss(out_col, ss_col, n, eps, rt, wt):
        dve(lambda e: e.tensor_scalar(out_col, ss_col, 1.0 / n, eps, ALU.mult, ALU.add), rt, wt)
        rsqrt_inplace(out_col, wt)

    def rsqrt_inplace(col, wt):
        act(col, col, AF.Sqrt, wt, wt)
        dve(lambda e: e.reciprocal(col, col), wt, wt)

    def psbf(b):
        return ps[:, b, :].bitcast(BF16)

    big_bf = big[:].rearrange("p t f -> p (t f)").bitcast(BF16)

    def bighi_f(t, a, b):
        return big[:, t, 1024 + a:1024 + b]

    def bighi_bf(t, a, b):
        base = t * 4096 + 2048
        return big_bf[:, base + a:base + b]

    t_ps = toks("ps", 8)
    t_xb = Tok("xb")
    t_hb = Tok("hb")
    t_gslot = Tok("gslot")

    def smc(i, n=1):
        return sm[:, i:i + n]

    dma("sp", identb[:], ident_d, [], [t_ident])
    P.add("pool", lambda e: e.memset(onesb[:], 1.0), [], [t_ones])
    P.add("pool", lambda e: e.memset(onesf[:], 1.0), [], [t_ones])
    g1c = sm[:, 256:272]
    g3c = sm[:, 272:288]
    subg = sm[:, 288:289]
    t_consts = Tok("consts")
    dma("sp", g1c, g1c_d, [], [t_consts])
    dma("sp", g3c, g3c_d, [], [t_consts])
    dma("sp", subg, subg_d, [], [t_consts])
    dma("sp", lamb[:], lam_d.partition_broadcast(128).rearrange("p o f -> p (o f)"), [], [t_consts])
    dma("sp", posall[:], posall_d, [], [t_consts])
    invf = sm[:, 300:332]
    dma("sp", invf, invf_d.partition_broadcast(128).rearrange("p o f -> p (o f)"), [], [t_consts])

    t_lam = Tok("lam")
    lt = sm[:, 340:404]
    dve(lambda e: e.tensor_tensor(lt, lamb[:, 0:64], lamb[:, 64:128], ALU.mult), [t_consts], [t_lam])
    dve(lambda e: e.reduce_sum(smc(290), lt, AX.X), [t_lam], [t_lam])
    dve(lambda e: e.tensor_tensor(lt, lamb[:, 128:192], lamb[:, 192:256], ALU.mult), [t_consts], [t_lam])
    dve(lambda e: e.reduce_sum(smc(291), lt, AX.X), [t_lam], [t_lam])
    act(sm[:, 292:294], sm[:, 290:292], AF.Exp, [t_lam], [t_lam])
    neglam = smc(294)
    dve(lambda e: e.tensor_tensor(neglam, smc(293), smc(292), ALU.subtract), [t_lam], [t_lam])
    dve(lambda e: e.tensor_scalar(neglam, neglam, -LAMBDA_INIT, None, ALU.add), [t_lam], [t_lam])
    gsc = smc(295)
    dve(lambda e: e.tensor_scalar(gsc, subg, 1.0 - LAMBDA_INIT, None, ALU.mult), [t_consts, t_lam], [t_lam])

    t_rope = Tok("rope")
    cosT = bighi_f(0, 0, 512).rearrange("p (t d) -> p t d", d=64)
    sinS = bighi_f(0, 512, 1024).rearrange("p (t d) -> p t d", d=64)
    posf = sm[:, 410:418]
    ang = sm[:, 0:256].rearrange("p (t d) -> p t d", d=32)
    tmpa = tmpa_t[:].rearrange("p (t d) -> p t d", d=32)
    SC = 1.0 - 2e-6
    C1 = 6.28125
    C2 = 2 * PI - C1
    kint = posi2[:].rearrange("p (t d) -> p t d", d=32)
    msk = lamb[:, 0:256].rearrange("p (t d) -> p t d", d=32)

    def wrap_pi(r):
        dve(lambda e: e.tensor_scalar(msk, r, -PI, 1e12, ALU.add, ALU.mult), [t_rope], [t_rope])
        dve(lambda e: e.tensor_scalar(msk, msk, 0.0, 1.0, ALU.max, ALU.min), [t_rope], [t_rope])
        dve(lambda e: e.scalar_tensor_tensor(r, msk, -2 * PI, r, ALU.mult, ALU.add), [t_rope], [t_rope])
        dve(lambda e: e.tensor_scalar(msk, r, PI, -1e12, ALU.add, ALU.mult), [t_rope], [t_rope])
        dve(lambda e: e.tensor_scalar(msk, msk, 0.0, 1.0, ALU.max, ALU.min), [t_rope], [t_rope])
        dve(lambda e: e.scalar_tensor_tensor(r, msk, 2 * PI, r, ALU.mult, ALU.add), [t_rope], [t_rope])

    def rope_tables(pos_i, cos_dst, sin_dst, wt):
        rw = [t_rope] + wt
        dve(lambda e: e.tensor_copy(posf, pos_i), [t_consts, t_lam], [t_rope])
        dve(lambda e: e.tensor_tensor(ang, posf.unsqueeze(2).broadcast_to([128, NT, 32]),
                                      invf.unsqueeze(1).broadcast_to([128, NT, 32]), ALU.mult),
            [t_consts], [t_rope])
        dve(lambda e: e.tensor_scalar(tmpa, ang, 1.0 / (2 * PI), None, ALU.mult), [t_rope], [t_rope])
        dve(lambda e: e.tensor_copy(kint, tmpa), [t_rope], [t_rope])
        dve(lambda e: e.tensor_copy(tmpa, kint), [t_rope], [t_rope])
        dve(lambda e: e.scalar_tensor_tensor(ang, tmpa, -C1, ang, ALU.mult, ALU.add), [t_rope], [t_rope])
        dve(lambda e: e.scalar_tensor_tensor(ang, tmpa, -C2, ang, ALU.mult, ALU.add), [t_rope], [t_rope])
        wrap_pi(ang)
        dve(lambda e: e.tensor_scalar(tmpa, ang, SC, None, ALU.mult), [t_rope], [t_rope])
        act(sin_dst[:, :, 32:64], tmpa, AF.Sin, [t_rope], rw)
        dve(lambda e: e.tensor_scalar(ang, ang, 0.5 * PI, None, ALU.add), [t_rope], [t_rope])
        wrap_pi(ang)
        dve(lambda e: e.tensor_scalar(tmpa, ang, SC, None, ALU.mult), [t_rope], [t_rope])
        act(cos_dst[:, :, 0:32], tmpa, AF.Sin, [t_rope], rw)
        dve(lambda e: e.tensor_scalar(sin_dst[:, :, 0:32], sin_dst[:, :, 32:64], -1.0, None, ALU.mult), rw, rw)
        dve(lambda e: e.tensor_copy(cos_dst[:, :, 32:64], cos_dst[:, :, 0:32]), rw, rw)

    t_ropeA = Tok("ropeA")
    cosA = [big[:, T // 16, (T % 16) * 64:(T % 16 + 1) * 64] for T in range(S // 128)]
    sinA = [big[:, 4 + T // 16, (T % 16) * 64:(T % 16 + 1) * 64] for T in range(S // 128)]
    def rope_tables_all(bch):
        T0 = bch * 8
        cdst = big[:, T0 // 16, (T0 % 16) * 64:(T0 % 16 + 8) * 64].rearrange("p (t d) -> p t d", d=64)
        sdst = big[:, 4 + T0 // 16, (T0 % 16) * 64:(T0 % 16 + 8) * 64].rearrange("p (t d) -> p t d", d=64)
        rope_tables(posall[:, T0:T0 + 8], cdst, sdst, [t_ropeA])

    rope_tables_all(0)

    t_gc = Tok("gmlpc")
    lnG = bighi_f(4, 0, 1024)
    lnB = bighi_f(5, 0, 1024)
    bsb = bighi_f(6, 0, 1024)
    wsT = bighi_bf(7, 0, 1024)
    dma("sp", lnG, lng_d.partition_broadcast(128).rearrange("p o f -> p (o f)"), [], [t_gc])
    dma("sp", lnB, lnb_d.partition_broadcast(128).rearrange("p o f -> p (o f)"), [], [t_gc])
    dma("sp", bsb, bs_d.partition_broadcast(128).rearrange("p o f -> p (o f)"), [], [t_gc])
    dma("pool", wsT, wsT_d, [], [t_gc])

    trc = [0]

    def to_feature_major(dst_fn, gcol, dst_toks, src=None, src_tok=None):
        src = hb if src is None else src
        src_tok = t_hb if src_tok is None else src_tok
        for half in range(2):
            b = 6 + (trc[0] % 2)
            trc[0] += 1
            pb = psbf(b)
            for j in range(8):
                k = half * 8 + j
                tr(pb[:, j * 128:(j + 1) * 128], src[:, k * 128:(k + 1) * 128], [src_tok], [t_ps[b]])
            for j in range(8):
                k = half * 8 + j
                o = dst_fn(k)
                i = pb[:, j * 128:(j + 1) * 128]
                sc = gcol[:, k:k + 1]
                P.add("dve", lambda e, o=o, i=i, sc=sc: e.tensor_scalar(o, i, sc, None, ALU.mult),
                      [t_ps[b], t_consts], dst_toks, relaxed=dst_toks)

    t_scr = toks("scr", 6)
    wbuf = [scr[:, 0:2, :].rearrange("p a (k f) -> p (a k) f", f=512),
            scr[:, 2:4, :].rearrange("p a (k f) -> p (a k) f", f=512)]
    t_wbuf = [[t_scr[0], t_scr[1]], [t_scr[2], t_scr[3]]]
    qT = scr[:, 4:6, :].rearrange("p a (h t) -> p (a h) t", t=1024)
    t_qT = [t_scr[4], t_scr[5]]
    win_v = win_d.rearrange("(k p) f -> p k f", p=128)

    def load_win(n, slot):
        dma("pool", wbuf[slot % 2], win_v[:, :, n * 512:(n + 1) * 512], [], t_wbuf[slot % 2])

    t_t1 = Tok("ropetmp")
    t1 = bighi_f(1, 0, 512).rearrange("p (m d) -> p m d", d=64)
    t2 = bighi_f(1, 512, 1024).rearrange("p (m d) -> p m d", d=64)
    qkb = [bighi_bf(2, 0, 512), bighi_bf(2, 512, 1024)]
    t_qkb = toks("qkb", 2)
    vln = bighi_bf(2, 1024, 2048)
    t_vln = Tok("vln")
    kst = [bighi_bf(3, 0, 512), bighi_bf(3, 512, 1024)]
    t_kst = toks("kst", 2)
    vst = [bighi_bf(3, 1024, 1536), bighi_bf(3, 1536, 2048)]
    t_vst = toks("vst", 2)
    t_lnst_l = toks("lnst", NT)


    bankc = [0]
    cnt_qk = [0]
    cnt_v = [0]

    def rope_only(b, cos_ap, sin_ap, cos_tok):
        x3 = ps[:, b, :].rearrange("p (m d) -> p m d", d=64)
        cb = cos_ap.unsqueeze(1).broadcast_to([128, 8, 64])
        s_lo = sin_ap[:, 0:32].unsqueeze(1).broadcast_to([128, 8, 32])
        s_hi = sin_ap[:, 32:64].unsqueeze(1).broadcast_to([128, 8, 32])
        i = cnt_qk[0] % 2
        cnt_qk[0] += 1
        dve(lambda e: e.tensor_tensor(t1, x3, cb, ALU.mult), [t_ps[b], cos_tok], [t_t1])
        dve(lambda e: e.tensor_tensor(t2[:, :, 0:32], x3[:, :, 32:64], s_lo, ALU.mult), [t_ps[b], cos_tok], [t_t1])
        dve(lambda e: e.tensor_tensor(t2[:, :, 32:64], x3[:, :, 0:32], s_hi, ALU.mult), [t_ps[b], cos_tok], [t_t1])
        dve(lambda e: e.tensor_tensor(qkb[i], bighi_f(1, 0, 512), bighi_f(1, 512, 1024), ALU.add),
            [t_t1], [t_qkb[i]])
        return i

    def tr4(i):
        pbi = 6 + (trc[0] % 2)
        trc[0] += 1
        pb = psbf(pbi)
        for j in range(4):
            tr(pb[:, j * 128:(j + 1) * 128], qkb[i][:, j * 128:(j + 1) * 128], [t_qkb[i]], [t_ps[pbi]])
        return pbi, pb[:, 0:512].rearrange("p (j t) -> p j t", t=128)

    def rope_to_bf16(b, cos_ap, sin_ap, cos_tok):
        i = rope_only(b, cos_ap, sin_ap, cos_tok)
        pbi, src = tr4(i)
        return i, pbi, src

    t_WKq = toks("WK", 4)
    t_WVq = toks("WV", 4)
    hTt = [scr[:, 4, 0:2048].rearrange("p (k t) -> p k t", t=128),
           scr[:, 4, 2048:4096].rearrange("p (k t) -> p k t", t=128)]
    t_hTt = [Tok("hTt0"), Tok("hTt1")]
    junk5 = scr[:, 5, 0:2048]
    kall_v = k_all.ap().rearrange("(h p) t -> p h t", p=128)
    t_kall = Tok("kall")
    t_vall = Tok("vall")
    t_ssK = Tok("ssK")
    NTA = S // 128

    xbufs = [xb, gslot]
    t_xbufs = [t_xb, t_gslot]

    def kv_load(T):
        dma("pool", xbufs[T % 2][:], xall_d[T * 128:(T + 1) * 128, :], [], [t_xbufs[T % 2]])

    t_ssKp = [Tok("ssK0"), Tok("ssK1")]

    def kv_norm(T):
        xs, tx = xbufs[T % 2], t_xbufs[T % 2]
        c0 = 500 + 2 * (T % 2)
        tss = t_ssKp[T % 2]
        act(junk5, xs[:], AF.Square, [tx], [tss], accum=smc(c0))
        rstd_from_ss(smc(c0 + 1), smc(c0), D, RMS_EPS, [tss], [tss])
        dve(lambda e: e.tensor_scalar(hb[:], xs[:], smc(c0 + 1), None, ALU.mult), [tx, tss], [t_hb])

    def kv_tr(T):
        i2 = T % 2
        for half in range(2):
            b = 6 + (trc[0] % 2)
            trc[0] += 1
            pb = psbf(b)
            for j in range(8):
                k = half * 8 + j
                tr(pb[:, j * 128:(j + 1) * 128], hb[:, k * 128:(k + 1) * 128], [t_hb], [t_ps[b]])
            dst = hTt[i2][:, half * 8:(half + 1) * 8, :]
            src = pb.rearrange("p (k t) -> p k t", t=128)
            if half == 0:
                act(dst, src, AF.Copy, [t_ps[b]], [t_hTt[i2]])
            else:
                dve(lambda e, dst=dst, src=src: e.tensor_copy(dst, src), [t_ps[b]], [t_hTt[i2]])
        if T < NT:
            hg = hTg[T % 2]
            thg = t_hTg[T % 2]
            for k in range(KC):
                P.add("dve", lambda e, k=k: e.tensor_scalar(hg[:, k, :], hTt[i2][:, k, :], g1c[:, k:k + 1], None,
                                                           ALU.mult),
                      [t_hTt[i2], t_consts], [thg], relaxed=[thg])
            dma("sp", hT_scr[:, :, T * 128:(T + 1) * 128], hg, [thg], [t_hTscr])

    hTg = [scr[:, 5, 2048:4096].rearrange("p (k t) -> p k t", t=128)] * 2
    t_hTg = [t_scr[5], t_scr[5]]
    t_hTscr = Tok("hTscr")
    kpend = {}

    def kv_mmK(T):
        i2 = T % 2
        kb = []
        for c in range(2):
            b = bankc[0] % 4
            bankc[0] += 1
            for k in range(KC):
                mm(ps[:, b, :], hTt[i2][:, k, :], hT[:, k, c * 512:(c + 1) * 512],
                   k == 0, k == KC - 1, [t_hTt[i2], t_WKq[k // 4]], [t_ps[b]])
            kb.append(b)
        kpend[T] = [rope_only(kb[c], cosA[T], sinA[T], t_ropeA) for c in range(2)]

    def kv_mmV(T):
        i2 = T % 2
        for c in range(2):
            b = bankc[0] % 4
            bankc[0] += 1
            for k in range(KC):
                mm(ps[:, b, :], hTt[i2][:, k, :], catT[:, k, c * 512:(c + 1) * 512],
                   k == 0, k == KC - 1, [t_hTt[i2], t_WVq[k // 4]], [t_ps[b]])
            i = cnt_v[0] % 2
            cnt_v[0] += 1
            act(vst[i], ps[:, b, :], AF.Copy, [t_ps[b]], [t_vst[i]])
            dma("sp", v_all.ap()[T * 128:(T + 1) * 128, c * 512:(c + 1) * 512], vst[i], [t_vst[i]], [t_vall])

    def kv_ktr(T):
        for c in range(2):
            i = kpend[T][c]
            pbi, src = tr4(i)
            kk = kst[i].rearrange("p (j t) -> p j t", t=128)
            act(kk, src, AF.Copy, [t_ps[pbi]], [t_kst[i]])
            dma("sp", kall_v[:, c * 4:(c + 1) * 4, T * 128:(T + 1) * 128], kk, [t_kst[i]], [t_kall])

    kv_load(0)
    kv_load(1)
    for c4 in range(4):
        dma("pool", hT[:, c4 * 4:(c4 + 1) * 4, :], win_v[:, c4 * 4:(c4 + 1) * 4, 1024:2048], [], [t_WKq[c4]])
    for c4 in range(4):
        dma("pool", catT[:, c4 * 4:(c4 + 1) * 4, :], win_v[:, c4 * 4:(c4 + 1) * 4, 2048:3072], [], [t_WVq[c4]])
    kv_norm(0)
    kv_tr(0)
    for k in range(KC):
        dve(lambda e, k=k: e.tensor_scalar(hT[:, k, :], hT[:, k, :], g1c[:, k:k + 1], None, ALU.mult),
            [t_WKq[k // 4], t_consts], [t_WKq[k // 4]])

    def fold_wv():
        for k in range(KC):
            if k % 2 == 0:
                act(catT[:, k, :], catT[:, k, :], AF.Copy, [t_WVq[k // 4], t_consts], [t_WVq[k // 4]],
                    scale=g1c[:, k:k + 1])
            else:
                dve(lambda e, k=k: e.tensor_scalar(catT[:, k, :], catT[:, k, :], g1c[:, k:k + 1], None, ALU.mult),
                    [t_WVq[k // 4], t_consts], [t_WVq[k // 4]])

    for T in range(NTA):
        if T % 8 == 0 and T + 8 < NTA:
            rope_tables_all(T // 8 + 1)
        if T + 1 < NTA:
            kv_norm(T + 1)
        if T + 2 < NTA:
            kv_load(T + 2)
        kv_mmK(T)
        if T == 0:
            fold_wv()
        if T + 1 < NTA:
            kv_tr(T + 1)
        kv_mmV(T)
        kv_ktr(T)

    t_scr[4] = Tok("scr4b", over=t_hTt)
    t_qT = [t_scr[4], t_scr[5]]
    t_gvg = [Tok(f"gvg{i}", over=[t_ropeA]) for i in range(NT)]
    t_cat = [[Tok(f"cat{j}_{c}", over=t_WVq) for c in range(2)] for j in range(KC)]

    t_hT = [Tok(f"hT{i}", over=t_WKq) for i in range(NT)]
    t_ssA_l = toks("ssA", NT)
    def a_load(t):
        dma("pool", xbufs[t % 2][:], x_d[t * 128:(t + 1) * 128, :], [], [t_xbufs[t % 2]])

    for q4 in range(4):
        dma("sp", hT[:, q4 * 4:(q4 + 1) * 4, :], hT_scr[:, q4 * 4:(q4 + 1) * 4, :], [t_hTscr], t_hT)

    dbg_items = []

    def finish():
        if dbg_d is not None:
            off = 0
            for (ap, n, rt) in dbg_items:
                dma("sp", dbg_d[:, off:off + n], ap, rt, [])
                off += n
        P.finalize()

        @block.tensor
        def _(e):
            P.emit("pe", e, engsem, dmasem)

        @block.scalar
        def _(e):
            P.emit("act", e, engsem, dmasem)

        @block.vector
        def _(e):
            P.emit("dve", e, engsem, dmasem)

        @block.gpsimd
        def _(e):
            P.emit("pool", e, engsem, dmasem)

        @block.sync
        def _(e):
            P.emit("sp", e, engsem, dmasem)

        es.close()
        return nc

    if stop_after == "A":
        if dbg_d is not None:
            dbg_items.append((hT[:, 0, :].bitcast(F32), 512, t_hT))
        return finish()

    ytmp = bighi_f(1, 0, 1024)

    vlnb = [vln, bighi_bf(3, 0, 1024)]
    t_vlnb = [t_vln, Tok("vln2", over=t_kst)]

    def gmlp_ln(t):
        gv = big[:, t, 0:1024]
        s1 = smc(440 + 2 * t, 2)
        mean = smc(460 + t)
        ssq = smc(470 + t)
        var = smc(480 + t)
        rs = smc(490 + t)
        act(hb[:, 0:1024], gv, AF.Square, [t_gvg[t]], [t_hb, t_lnst_l[t]], accum=ssq)
        dve(lambda e, mean=mean, s1=s1: e.reduce_sum(mean, s1, AX.X), [t_lnst_l[t]], [t_lnst_l[t]])
        dve(lambda e, mean=mean: e.tensor_scalar(mean, mean, 1.0 / 1024, None, ALU.mult), [t_lnst_l[t]], [t_lnst_l[t]])
        dve(lambda e, var=var, mean=mean: e.tensor_tensor(var, mean, mean, ALU.mult), [t_lnst_l[t]], [t_lnst_l[t]])
        dve(lambda e, var=var, ssq=ssq: e.scalar_tensor_tensor(var, ssq, 1.0 / 1024, var, ALU.mult, ALU.subtract),
            [t_lnst_l[t]], [t_lnst_l[t]])
        dve(lambda e, var=var, rs=rs: e.tensor_scalar(rs, var, LN_EPS, None, ALU.add), [t_lnst_l[t]], [t_lnst_l[t]])
        rsqrt_inplace(rs, [t_lnst_l[t]])
        dve(lambda e, gv=gv, mean=mean, rs=rs: e.tensor_scalar(gv, gv, mean, rs, ALU.subtract, ALU.mult),
            [t_lnst_l[t], t_gvg[t]], [t_gvg[t]])
        dve(lambda e, gv=gv: e.tensor_tensor(gv, gv, lnG, ALU.mult), [t_gvg[t], t_gc], [t_gvg[t]])
        vl, tvl = vlnb[t % 2], t_vlnb[t % 2]
        dve(lambda e, gv=gv, vl=vl: e.tensor_tensor(vl, gv, lnB, ALU.add), [t_gvg[t], t_gc], [tvl])

    def gmlp_mm(t):
        vl, tvl = vlnb[t % 2], t_vlnb[t % 2]
        for g in range(8):
            b = 4 + g // 4
            mm(ps[:, b, (g % 4) * 128:(g % 4 + 1) * 128], vl[:, g * 128:(g + 1) * 128],
               wsT[:, g * 128:(g + 1) * 128], True, True, [tvl, t_gc], [t_ps[b]])
        yv = ps[:, 4:6, :].rearrange("p a b -> p (a b)")
        dve(lambda e, yv=yv: e.tensor_tensor(ytmp, yv, bsb, ALU.add), [t_ps[4], t_ps[5], t_gc], [t_t1])
        tc = t // 4
        dve(lambda e, t=t: e.tensor_tensor(catT[:, 8:16, t * 128:(t + 1) * 128],
                                           ytmp.rearrange("p (g c) -> p g c", c=128),
                                           catT[:, 0:8, t * 128:(t + 1) * 128], ALU.mult),
            [t_t1] + [t_cat[j][tc] for j in range(8)], [t_cat[8 + j][tc] for j in range(8)])


    qprev = [None]

    def q_tr(i, n, t):
        pbi, src = tr4(i)
        dst = qT[:, n * 4:(n + 1) * 4, t * 128:(t + 1) * 128]
        act(dst, src, AF.Copy, [t_ps[pbi]], [t_qT[n]])

    chunks = [0, 1, 6, 7, 8, 9]
    load_win(chunks[0], 0)
    load_win(chunks[1], 1)
    for ci, n in enumerate(chunks):
        wb = wbuf[ci % 2]
        twb = t_wbuf[ci % 2]
        if n in (6, 7):
            for fc in range(4):
                j = (n - 6) * 4 + fc
                for tc in range(2):
                    b = bankc[0] % 4
                    bankc[0] += 1
                    for k in range(KC):
                        mm(ps[:, b, :], wb[:, k, fc * 128:(fc + 1) * 128], hT[:, k, tc * 512:(tc + 1) * 512],
                           k == 0, k == KC - 1, twb + t_hT[tc * 4:(tc + 1) * 4], [t_ps[b]])
                    act(catT[:, j, tc * 512:(tc + 1) * 512], ps[:, b, :], AF.Gelu, [t_ps[b]], [t_cat[j][tc]])
        else:
            for t in range(NT):
                b = bankc[0] % 4
                bankc[0] += 1
                for k in range(KC):
                    mm(ps[:, b, :], hT[:, k, t * 128:(t + 1) * 128], wb[:, k, :],
                       k == 0, k == KC - 1, twb + [t_hT[t]], [t_ps[b]])
                if n < 2:
                    i = rope_only(b, cosA[t], sinA[t], t_ropeA)
                    if qprev[0] is not None:
                        q_tr(*qprev[0])
                    qprev[0] = (i, n, t)
                else:
                    c = n - 8
                    act(big[:, t, c * 512:(c + 1) * 512], ps[:, b, :], AF.Gelu, [t_ps[b]],
                        [t_gvg[t], t_lnst_l[t]], accum=smc(440 + t * 2 + c))
                    if n == 9:
                        if t >= 2:
                            gmlp_mm(t - 2)
                        gmlp_ln(t)
            if n < 2 and qprev[0] is not None:
                q_tr(*qprev[0])
                qprev[0] = None
            if n == 9:
                gmlp_mm(NT - 2)
                gmlp_mm(NT - 1)
        if ci + 2 < len(chunks):
            load_win(chunks[ci + 2], ci)

    if stop_after == "B":
        if dbg_d is not None:
            dbg_items.append((hT[:, 0, :].bitcast(F32), 512, t_hT))
            dbg_items.append((qT[:, 0, :].bitcast(F32), 512, t_qT))
            dbg_items.append((catT[:, 8, :].bitcast(F32), 512, [t_cat[8][0], t_cat[8][1]]))
            dbg_items.append((catT[:, 0, :].bitcast(F32), 512, [t_cat[0][0], t_cat[0][1]]))
            dbg_items.append((sm[:, 0:512], 512, [t_lam, t_rope] + t_lnst_l + t_ssA_l))
        return finish()

    all_big = t_gvg + [t_rope, t_ropeA, t_gc, t_t1, t_vln] + t_qkb + t_kst + t_vst + t_vlnb
    t_KT = [Tok("KT0", over=t_hT), Tok("KT1", over=t_hT)]
    KTb = [hT[:, 0:8, :], hT[:, 8:16, :]]
    t_V = [Tok("V0", over=all_big), Tok("V1", over=all_big)]
    Vb = [big_bf[:, 4 * 4096:6 * 4096].rearrange("p (k d) -> p k d", d=128),
          big_bf[:, 6 * 4096:8 * 4096].rearrange("p (k d) -> p k d", d=128)]
    NE = 4
    Eb = [big_bf[:, i * 1024:(i + 1) * 1024] for i in range(NE)]
    t_E = [Tok(f"E{i}", over=all_big) for i in range(NE)]
    t_fin = Tok("fin", over=all_big)
    r1 = big[:, 2, 0:512]
    r2 = big[:, 2, 512:1024]
    o1 = big[:, 2, 1024:1536]
    o2 = big[:, 2, 1536:2048]
    rsn = big[:, 3, 0:512]
    sqb = big[:, 3, 512:768].bitcast(BF16)
    esum = big[:, 3, 1024:2048]
    t_esum = Tok("esum", over=all_big)
    t_esum1 = Tok("esum1", over=all_big)
    SB = [0, 2, 6]
    kvo = k_all.ap().rearrange("x (r t) -> x r t", r=NCORES)

    ec = [0]
    blocks = [(h, qc) for h in range(8) for qc in range(2)]

    def load_head(h):
        dma("sp", KTb[h % 2], kvo[h * 128:(h + 1) * 128, :, :], [t_kall], [t_KT[h % 2]])
        for r in range(NCORES):
            src = v_all.ap()[r * 1024:(r + 1) * 1024, h * 128:(h + 1) * 128]
            dma("sp", Vb[h % 2][:, r * 8:(r + 1) * 8, :], src.rearrange("(k p) d -> p k d", p=128),
                [t_vall], [t_V[h % 2]])

    def scores(h, qc, kt):
        KT, tk, tq = KTb[h % 2], t_KT[h % 2], t_qT[h // 4]
        qs = slice(qc * 512, (qc + 1) * 512)
        b0 = 2 * (kt % 2)
        ksl = (kt // 8, slice((kt % 8) * 128, (kt % 8 + 1) * 128))
        mm(ps[:, b0, :], KT[0:64, ksl[0], ksl[1]], qT[0:64, h, qs], True, True, [tk, tq], [t_ps[b0]])
        mm(ps[:, b0 + 1, :], KT[64:128, ksl[0], ksl[1]], qT[64:128, h, qs], True, True,
           [tk, tq], [t_ps[b0 + 1]])

    def stage_a0(h, qc):
        dve(lambda e: e.tensor_copy(o1, ps[:, 4, :]), [t_ps[4]], [t_fin])
        dve(lambda e: e.tensor_copy(o2, ps[:, 5, :]), [t_ps[5]], [t_fin])
        dve(lambda e: e.tensor_copy(esum[0:1, 0:512], ps[0:1, 6, :]), [t_ps[6]], [t_esum])
        dve(lambda e: e.tensor_copy(esum[32:33, 0:512], ps[32:33, 6, :]), [t_ps[6]], [t_esum])

    s_row = [esum[0:1, 0:512], esum[32:33, 0:512]]
    rr_row = [esum[0:1, 512:1024], esum[32:33, 512:1024]]
    hi_row = [big[0:1, 2, 0:256].bitcast(BF16), big[32:33, 2, 0:256].bitcast(BF16)]
    lo_row = [big[0:1, 2, 256:512].bitcast(BF16), big[32:33, 2, 256:512].bitcast(BF16)]
    tmp_row = [big[0:1, 2, 512:1024], big[32:33, 2, 512:1024]]
    one_row = [onesb[0:1, :], onesb[32:33, :]]
    t_rows = Tok("rows", over=all_big)

    def stage_a1(h, qc):
        for m in range(2):
            dve(lambda e, m=m: e.reciprocal(rr_row[m], s_row[m]), [t_esum], [t_rows])
            dve(lambda e, m=m: e.tensor_copy(hi_row[m], rr_row[m]), [t_rows], [t_rows])
            dve(lambda e, m=m: e.tensor_tensor(tmp_row[m], rr_row[m], hi_row[m], ALU.subtract), [t_rows], [t_rows])
            dve(lambda e, m=m: e.tensor_copy(lo_row[m], tmp_row[m]), [t_rows], [t_rows])

    def bcast_norm(m, o):
        mm(ps[:, 7, :], one_row[m], hi_row[m], True, False, [t_ones, t_rows], [t_ps[7]])
        mm(ps[:, 7, :], one_row[m], lo_row[m], False, True, [t_ones, t_rows], [t_ps[7]])
        dve(lambda e: e.tensor_tensor(o, o, ps[:, 7, :], ALU.mult), [t_ps[7], t_fin], [t_fin])

    def stage_a2(h, qc):
        bcast_norm(0, o1)

    def stage_a3(h, qc):
        bcast_norm(1, o2)
        dve(lambda e: e.scalar_tensor_tensor(o1, o2, neglam, o1, ALU.mult, ALU.add), [t_fin, t_lam], [t_fin])
        dve(lambda e: e.tensor_tensor(sqb, o1, o1, ALU.mult), [t_fin], [t_fin])

    def stage_b1(h, qc):
        mm(ps[:, 7, :], onesb[:], sqb, True, True, [t_ones, t_fin], [t_ps[7]])
        dve(lambda e: e.tensor_scalar(rsn, ps[:, 7, :], 1.0 / 128, SUBLN_EPS, ALU.mult, ALU.add),
            [t_ps[7]], [t_fin])

    def stage_b2(h, qc):
        qs = slice(qc * 512, (qc + 1) * 512)
        act(rsn, rsn, AF.Ln, [t_fin], [t_fin])
        act(rsn, rsn, AF.Exp, [t_fin], [t_fin], scale=-0.5)
        dve(lambda e: e.scalar_tensor_tensor(catT[:, h, qs], o1, gsc, rsn, ALU.mult, ALU.mult),
            [t_fin, t_lam], [t_cat[h][qc]])

    stages = {1: stage_a1, 6: stage_a2, 9: stage_a3, 13: stage_b1, 17: stage_b2}

    pending = [None]
    load_head(0)
    scores(0, 0, 0)
    for bi, (h, qc) in enumerate(blocks):
        if qc == 0 and h + 1 < 8:
            load_head(h + 1)
        V, tv = Vb[h % 2], t_V[h % 2]
        for kt in range(64):
            b0 = 2 * (kt % 2)
            if kt + 1 < 64:
                scores(h, qc, kt + 1)
            ei = ec[0] % NE
            ec[0] += 1
            E = Eb[ei]
            act(E, ps[:, b0:b0 + 2, :].rearrange("p a b -> p (a b)"), AF.Exp,
                [t_ps[b0], t_ps[b0 + 1]], [t_E[ei]], scale=0.125)
            st, sp_ = kt == 0, kt == 63
            mm(ps[0:32, 6, :], onesb[:, 0:32], E[:, 0:512], st, sp_, [t_ones, t_E[ei]], [t_ps[6]], tp=(0, 0))
            mm(ps[32:64, 6, :], onesb[:, 0:32], E[:, 512:1024], st, sp_, [t_ones, t_E[ei]], [t_ps[6]],
               tp=(0, 32))
            mm(ps[:, 4, :], V[:, kt, :], E[:, 0:512], st, sp_, [tv, t_E[ei]], [t_ps[4]])
            mm(ps[:, 5, :], V[:, kt, :], E[:, 512:1024], st, sp_, [tv, t_E[ei]], [t_ps[5]])
            if pending[0] is not None and kt in stages:
                stages[kt](*pending[0])
                if kt == 17:
                    pending[0] = None
        if bi + 1 < len(blocks):
            scores(blocks[bi + 1][0], blocks[bi + 1][1], 0)
        stage_a0(h, qc)
        pending[0] = (h, qc)
    for kt_ in sorted(stages):
        stages[kt_](*pending[0])

    if stop_after == "C":
        if dbg_d is not None:
            for j in range(8):
                dbg_items.append((catT[:, j, :].bitcast(F32), 512, [t_cat[j][0], t_cat[j][1]]))
        return finish()

    all_c = t_V + t_E + [t_fin]
    t_mix = [Tok(f"mix{t}", over=all_c) for t in range(NT)]
    t_wbufD = [[Tok("wD0a", over=t_scr[0:2]), Tok("wD0b")], [Tok("wD1a", over=t_scr[2:4]), Tok("wD1b")]]
    wout_v = wout_d.rearrange("(k p) f -> p k f", p=128)
    t_h2T = [Tok(f"h2T{t}", over=t_KT) for t in range(NT)]
    t_x1scr = toks("x1scr", NT)
    t_ssD_l = toks("ssD", NT)
    xbD = [xb[:], scr[:, 4, :].bitcast(F32)]
    t_xbD = [t_xb, Tok("xb2D", over=[t_scr[4]])]

    def load_wout(n):
        dma("pool", wbuf[n % 2], wout_v[:, :, n * 512:(n + 1) * 512], [], t_wbufD[n % 2])

    dma("sp", gslot[:], g2_d.partition_broadcast(128).rearrange("p o f -> p (o f)"), [], [t_gslot])
    load_wout(0)
    load_wout(1)
    for n in range(4):
        wb = wbuf[n % 2]
        for t in range(NT):
            b = bankc[0] % 4
            bankc[0] += 1
            tc = t // 4
            for k in range(KC):
                mm(ps[:, b, :], catT[:, k, t * 128:(t + 1) * 128], wb[:, k, :],
                   k == 0, k == KC - 1, t_wbufD[n % 2] + [t_cat[k][tc]], [t_ps[b]])
            act(big[:, t, n * 512:(n + 1) * 512], ps[:, b, :], AF.Copy, [t_ps[b]], [t_mix[t]])
        if n + 2 < 4:
            load_wout(n + 2)
    hbD = [hb, scr[:, 5, 2048:4096]]
    t_hbD = [t_hb, Tok("hb2D", over=[t_scr[5]])]

    def d_s1a(t):
        mx = big[:, t, :]
        act(junk5, mx, AF.Square, [t_mix[t]], [t_ssD_l[t]], accum=smc(96 + t))
        rstd_from_ss(smc(104 + t), smc(96 + t), D, RMS_EPS, [t_ssD_l[t]], [t_ssD_l[t]])
        xs, tx = xbD[t % 2], t_xbD[t % 2]
        dma("sp", xs, x_d[t * 128:(t + 1) * 128, :], [], [tx])
        dve(lambda e: e.scalar_tensor_tensor(mx, mx, smc(104 + t), gslot[:], ALU.mult, ALU.mult),
            [t_mix[t], t_ssD_l[t], t_gslot], [t_mix[t]])
        dve(lambda e: e.tensor_tensor(mx, mx, xs, ALU.add), [t_mix[t], tx], [t_mix[t]], eng="pool")
        dma("sp", x1_scr[t * 128:(t + 1) * 128, :], mx, [t_mix[t]], [t_x1scr[t]])
        act(junk5, mx, AF.Square, [t_mix[t]], [t_ssD_l[t]], accum=smc(112 + t))

    def d_s1b(t):
        mx = big[:, t, :]
        dve(lambda e: e.tensor_scalar(smc(120 + t), smc(112 + t), 1.0 / D, RMS_EPS, ALU.mult, ALU.add),
            [t_ssD_l[t]], [t_ssD_l[t]])
        act(smc(120 + t), smc(120 + t), AF.Sqrt, [t_ssD_l[t]], [t_ssD_l[t]])
        dve(lambda e: e.reciprocal(smc(120 + t), smc(120 + t)), [t_ssD_l[t]], [t_ssD_l[t]])
        dve(lambda e: e.tensor_scalar(hbD[t % 2][:], mx, smc(120 + t), None, ALU.mult),
            [t_mix[t], t_ssD_l[t]], [t_hbD[t % 2]])

    def d_s2(t):
        to_feature_major(lambda k, t=t: hT[:, k, t * 128:(t + 1) * 128], g3c, [t_h2T[t]],
                         src=hbD[t % 2], src_tok=t_hbD[t % 2])

    d_s1a(0)
    d_s1a(1)
    d_s1b(0)
    for t in range(NT):
        if t + 2 < NT:
            d_s1a(t + 2)
        d_s2(t)
        if t + 1 < NT:
            d_s1b(t + 1)

    if stop_after == "D":
        for t in range(NT):
            dma("sp", out_d[t * 128:(t + 1) * 128, :], big[:, t, :], [t_mix[t]], [])
        return finish()

    NG = DFF // 512
    t_f = [Tok(f"f{t}", over=[t_mix[t]]) for t in range(NT)]
    gub = [scr[:, i, :].rearrange("p (k f) -> p k f", f=256) for i in range(3)]
    t_gub = [Tok("gub0", over=t_wbufD[0]), Tok("gub1", over=t_wbufD[0]), Tok("gub2", over=t_wbufD[1])]
    actT = [scr[:, 3, :].rearrange("p (c t) -> p c t", t=1024), scr[:, 4, :].rearrange("p (c t) -> p c t", t=1024)]
    t_actT = [Tok("actT0", over=t_wbufD[1]), Tok("actT1", over=[t_scr[4]])]
    all_cat = [t_cat[j][c] for j in range(KC) for c in range(2)]
    wdb = [catT[:, 0:8, :].rearrange("p a t -> p (a t)").rearrange("p (c n) -> p c n", n=2048),
           catT[:, 8:16, :].rearrange("p a t -> p (a t)").rearrange("p (c n) -> p c n", n=2048)]
    t_wdb = [Tok("wdb0", over=all_cat), Tok("wdb1", over=all_cat)]
    sgt = [xb[:, 0:512], xb[:, 512:1024]]
    t_sgt = [Tok("sgt0", over=[t_xb]), Tok("sgt1", over=[t_xb])]
    wg_v = wg_d.rearrange("(k p) f -> p k f", p=128)
    wu_v = wu_d.rearrange("(k p) f -> p k f", p=128)
    wd_v = wd_d.rearrange("(c p) n -> p c n", p=128)

    pieces = [(g, c, w) for g in range(NG) for c in range(2) for w in range(2)]

    def load_piece(i):
        g, c, w = pieces[i]
        src = (wg_v if w == 0 else wu_v)[:, :, g * 512 + c * 256:g * 512 + (c + 1) * 256]
        dma("pool", gub[i % 3], src, [], [t_gub[i % 3]])

    def load_wd(g):
        dma("pool", wdb[g % 2], wd_v[:, g * 4:(g + 1) * 4, :], [], [t_wdb[g % 2]])

    dma("sp", gslot[:], g4_d.partition_broadcast(128).rearrange("p o f -> p (o f)"), [], [t_gslot])
    load_piece(0)
    load_piece(1)
    load_wd(0)
    sgc = [0]
    dbank = [0]
    def ffn_gu(g):
        at = actT[g % 2]
        tat = t_actT[g % 2]
        for c in range(2):
            ig = (g * 2 + c) * 2
            if ig + 2 < len(pieces):
                load_piece(ig + 2)
            wgb, tg = gub[ig % 3], t_gub[ig % 3]
            wub, tu = gub[(ig + 1) % 3], t_gub[(ig + 1) % 3]
            for fcl in range(2):
                fl = c * 2 + fcl
                for tc in range(2):
                    bg = tc
                    bu = 2 + tc
                    for k in range(KC):
                        mm(ps[:, bg, :], wgb[:, k, fcl * 128:(fcl + 1) * 128], hT[:, k, tc * 512:(tc + 1) * 512],
                           k == 0, k == KC - 1, [tg] + t_h2T[tc * 4:(tc + 1) * 4], [t_ps[bg]])
                    for k in range(KC):
                        mm(ps[:, bu, :], wub[:, k, fcl * 128:(fcl + 1) * 128], hT[:, k, tc * 512:(tc + 1) * 512],
                           k == 0, k == KC - 1, [tu] + t_h2T[tc * 4:(tc + 1) * 4], [t_ps[bu]])
                    si = sgc[0] % 2
                    sgc[0] += 1
                    act(sgt[si], ps[:, bg, :], AF.Silu, [t_ps[bg]], [t_sgt[si]])
                    dve(lambda e, at=at, fl=fl, tc=tc, si=si, bu=bu:
                        e.tensor_tensor(at[:, fl, tc * 512:(tc + 1) * 512], ps[:, bu, :], sgt[si], ALU.mult),
                        [t_ps[bu], t_sgt[si]], [tat])
            if ig + 3 < len(pieces):
                load_piece(ig + 3)

    def ffn_down(g):
        at = actT[g % 2]
        tat = t_actT[g % 2]
        if g + 1 < NG:
            load_wd(g + 1)
        wd = wdb[g % 2]
        twd = t_wdb[g % 2]
        if g == NG - 1:
            xbE = [xb[:], scr[:, 0, :].bitcast(F32)]
            t_xbE = [Tok("xbE0", over=t_sgt), Tok("xbE1", over=t_gub)]
            t_ssE_l = toks("ssE", NT)

            def epilogue(t):
                fv = big[:, t, :]
                xs, tx = xbE[t % 2], t_xbE[t % 2]
                act(junk5, fv, AF.Square, [t_f[t]], [t_ssE_l[t]], accum=smc(128 + t))
                rstd_from_ss(smc(136 + t), smc(128 + t), D, RMS_EPS, [t_ssE_l[t]], [t_ssE_l[t]])
                dma("sp", xs, x1_scr[t * 128:(t + 1) * 128, :], [t_x1scr[t]], [tx])
                dve(lambda e: e.scalar_tensor_tensor(fv, fv, smc(136 + t), gslot[:], ALU.mult, ALU.mult),
                    [t_f[t], t_ssE_l[t], t_gslot], [t_f[t]])
                dve(lambda e: e.tensor_tensor(fv, fv, xs, ALU.add), [t_f[t], tx], [t_f[t]], eng="pool")
                dma("sp", out_d[t * 128:(t + 1) * 128, :], fv, [t_f[t]], [])
        for t in range(NT):
            for n in range(4):
                b = 4 + dbank[0] % 4
                dbank[0] += 1
                for fl in range(4):
                    mm(ps[:, b, :], at[:, fl, t * 128:(t + 1) * 128], wd[:, fl, n * 512:(n + 1) * 512],
                       fl == 0, fl == 3, [tat, twd], [t_ps[b]])
                fv = big[:, t, n * 512:(n + 1) * 512]
                if g == 0:
                    dve(lambda e, fv=fv, b=b: e.tensor_copy(fv, ps[:, b, :]), [t_ps[b]], [t_f[t]])
                else:
                    dve(lambda e, fv=fv, b=b: e.tensor_tensor(fv, ps[:, b, :], fv, ALU.add), [t_ps[b], t_f[t]], [t_f[t]])
            if g == NG - 1:
                epilogue(t)

    ffn_gu(0)
    for g in range(NG):
        if g + 1 < NG:
            ffn_gu(g + 1)
        ffn_down(g)

    return finish()


def make_in_maps(x, positions, pre_mix_g, w_in, lambda_q1, lambda_k1, lambda_q2, lambda_k2,
                 subln_g, gmlp_ln_g, gmlp_ln_b, w_s, b_s, w_out, post_mix_g,
                 pre_ffn_g, w_gate, w_up, w_down, post_ffn_g):
    f = lambda a: np.ascontiguousarray(np.asarray(a, dtype=np.float32))
    x = f(x)[0]
    pos = np.asarray(positions, dtype=np.int32)[0]
    invf = (10000.0 ** (-(np.arange(0, 64, 2, dtype=np.float32) / np.float32(64)))).astype(np.float32)[None, :]
    ident = np.eye(128, dtype=np.float32).astype(ml_dtypes.bfloat16)
    common = {
        "invf": invf,
        "ident": ident,
        "g1c": np.ascontiguousarray(f(pre_mix_g)[0].reshape(KC, 128).T),
        "g3c": np.ascontiguousarray(f(pre_ffn_g)[0].reshape(KC, 128).T),
        "g2": f(post_mix_g)[0][None, :],
        "g4": f(post_ffn_g)[0][None, :],
        "lamv": np.concatenate([f(lambda_q1)[0], f(lambda_k1)[0], f(lambda_q2)[0], f(lambda_k2)[0]])[None, :],
        "subg": f(subln_g)[0][:, None],
        "lng": f(gmlp_ln_g)[0][None, :],
        "lnb": f(gmlp_ln_b)[0][None, :],
        "wsT": np.ascontiguousarray(f(w_s)[0].transpose(2, 0, 1).reshape(128, 1024)),
        "bs": f(b_s)[0].reshape(1, 1024),
        "w_in": f(w_in)[0],
        "w_out": f(w_out)[0],
        "w_gate": f(w_gate)[0],
        "w_up": f(w_up)[0],
        "w_down": f(w_down)[0],
    }
    in_maps = []
    for c in range(NCORES):
        m = dict(common)
        m["x"] = np.ascontiguousarray(x[c * TOK:(c + 1) * TOK])
        m["x_all"] = np.ascontiguousarray(np.roll(x, -c * TOK, axis=0))
        m["pos_all"] = np.ascontiguousarray(np.roll(pos, -c * TOK).reshape(S // 128, 128).T)
        m["pos"] = np.ascontiguousarray(pos[c * TOK:(c + 1) * TOK].reshape(NT, 128).T)
        in_maps.append(m)
    return in_maps


def kernel(**inputs):
    in_maps = make_in_maps(**inputs)
    nc = build_nc()
    res = run_bass_kernel_spmd(nc, in_maps, core_ids=list(range(NCORES)))
    out = np.concatenate([np.asarray(r["out"], dtype=np.float32) for r in res.results], axis=0)
    return out[None, :, :]
```

```python
import math
from contextlib import ExitStack
import numpy as np
import ml_dtypes
import concourse.bass as bass
import concourse.mybir as mybir
from concourse.bass_utils import run_bass_kernel_spmd

F32 = mybir.dt.float32
BF16 = mybir.dt.bfloat16
I32 = mybir.dt.int32
AF = mybir.ActivationFunctionType
ALU = mybir.AluOpType
AX = mybir.AxisListType

NCORES = 8
S = 8192
D = 2048
TOK = S // NCORES
NT = TOK // 128
KC = D // 128
DFF = 5632
INW = 5120
LAMBDA_INIT = 0.8 - 0.6 * math.exp(-0.3 * 0)
RMS_EPS = 1e-6
LN_EPS = 1e-5
SUBLN_EPS = 1e-5
PI = math.pi


class Tok:
    __slots__ = ("name", "w", "r")

    def __init__(self, name, over=()):
        self.name = name
        self.w = None
        self.r = {}
        for o in over:
            if o.w is not None:
                self.r[("w", id(o))] = o.w
            for k, v in o.r.items():
                self.r[("r", id(o), k)] = v


class Op:
    __slots__ = ("eng", "fn", "deps", "idx", "dma", "sem", "val", "inc", "need_sig", "sigval", "slot")

    def __init__(self, eng, fn, dma):
        self.eng = eng
        self.fn = fn
        self.deps = set()
        self.dma = dma
        self.sem = None
        self.val = 0
        self.inc = 16
        self.need_sig = False
        self.sigval = 0
        self.slot = None


class Prog:
    ENGS = ("pe", "act", "dve", "pool", "sp")
    NSLOT = {"sp": 8, "pool": 6}

    def __init__(self):
        self.ops = []
        self.dma_count = {"sp": 0, "pool": 0}
        self.slot_last = {}

    def add(self, eng, fn, reads=(), writes=(), dma=False, sem_override=None, relaxed=()):
        op = Op(eng, fn, dma)
        op.idx = len(self.ops)
        deps = set()
        for t in reads:
            if t.w is not None:
                deps.add(t.w)
        for t in writes:
            if t.w is not None and not (t in relaxed and t.w.eng == eng and not t.w.dma):
                deps.add(t.w)
            deps.update(t.r.values())
        if eng == "pe" and not dma:
            deps = {d for d in deps if not (d.eng == "pe" and not d.dma)}
        if dma:
            if sem_override is not None:
                op.sem, op.val, op.inc = sem_override
            else:
                k = self.dma_count[eng]
                self.dma_count[eng] = k + 1
                ns = self.NSLOT[eng]
                op.slot = (eng, k % ns)
                op.val = 16 * (k // ns + 1)
                prev = self.slot_last.get(op.slot)
                if prev is not None:
                    deps.add(prev)
                self.slot_last[op.slot] = op
        for d in deps:
            d.need_sig = True
        op.deps = deps
        key = ("dma", op.idx) if dma else eng
        for t in reads:
            t.r[key] = op
        for t in writes:
            t.w = op
            t.r = {}
        self.ops.append(op)
        return op

    def finalize(self):
        cnt = {e: 0 for e in self.ENGS}
        for op in self.ops:
            if not op.dma and op.need_sig:
                cnt[op.eng] += 1
                op.sigval = cnt[op.eng]
        return cnt

    def emit(self, eng_name, eng, engsem, dmasem):
        waited = {}
        for op in self.ops:
            if op.eng != eng_name:
                continue
            needs = {}
            for d in op.deps:
                if d.dma:
                    s = d.sem if d.sem is not None else dmasem[d.slot]
                    v = d.val
                else:
                    s = engsem[d.eng]
                    v = d.sigval
                key = s.num
                if needs.get(key, (None, 0))[1] < v:
                    needs[key] = (s, v)
            for key, (s, v) in needs.items():
                if waited.get(key, 0) < v:
                    eng.wait_ge(s, v)
                    waited[key] = v
            ins = op.fn(eng)
            if op.dma:
                s = op.sem if op.sem is not None else dmasem[op.slot]
                ins.then_inc(s, op.inc)
            elif op.need_sig:
                ins.then_inc(engsem[eng_name], 1)
        for slot, op in self.slot_last.items():
            if slot[0] == eng_name:
                s = dmasem[slot]
                if waited.get(s.num, 0) < op.val:
                    eng.wait_ge(s, op.val)
                    waited[s.num] = op.val


def toks(prefix, n):
    return [Tok(f"{prefix}{i}") for i in range(n)]


def build_nc(stop_after="E", dbg=None):
    nc = bass.Bass("TRN2", target_bir_lowering=False)
    dt = nc.dram_tensor

    def din(name, shape, dtype=F32):
        return dt(name, shape, dtype, kind="ExternalInput").ap()

    x_d = din("x", [TOK, D])
    xall_d = din("x_all", [S, D])
    pos_d = din("pos", [128, NT], I32)
    posall_d = din("pos_all", [128, S // 128], I32)
    invf_d = din("invf", [1, 32])
    ident_d = din("ident", [128, 128], BF16)
    g1c_d = din("g1c", [128, KC])
    g3c_d = din("g3c", [128, KC])
    g2_d = din("g2", [1, D])
    g4_d = din("g4", [1, D])
    lam_d = din("lamv", [1, 256])
    subg_d = din("subg", [128, 1])
    lng_d = din("lng", [1, 1024])
    lnb_d = din("lnb", [1, 1024])
    wsT_d = din("wsT", [128, 1024])
    bs_d = din("bs", [1, 1024])
    win_d = din("w_in", [D, INW])
    wout_d = din("w_out", [D, D])
    wg_d = din("w_gate", [D, DFF])
    wu_d = din("w_up", [D, DFF])
    wd_d = din("w_down", [DFF, D])
    out_d = dt("out", [TOK, D], F32, kind="ExternalOutput").ap()
    dbg_d = None
    if dbg:
        dbg_d = dt("dbg", [128, 16 * 1024], F32, kind="ExternalOutput").ap()

    k_all = dt("k_all", [1024, S], BF16)
    v_all = dt("v_all", [S, 1024], BF16)
    x1_scr = dt("x1_scr", [TOK, D], F32).ap()
    hT_scr = dt("hT_scr", [128, KC, TOK], BF16).ap()

    P = Prog()

    es = ExitStack()
    sb = lambda name, shape, dtype: es.enter_context(nc.sbuf_tensor(name, shape, dtype))
    hT = sb("hT", [128, KC, TOK], BF16)
    catT = sb("catT", [128, KC, TOK], BF16)
    big = sb("big", [128, NT, 2048], F32)
    scr = sb("scr", [128, 6, 4096], BF16)
    xb = sb("xb", [128, D], F32)
    gslot = sb("gslot", [128, D], F32)
    hb = sb("hb", [128, D], BF16)
    sm = sb("sm", [128, 512], F32)
    lamb = sb("lamb", [128, 256], F32)
    tmpa_t = sb("tmpa", [128, 256], F32)
    identb = sb("identb", [128, 128], BF16)
    onesb = sb("onesb", [128, 128], BF16)
    onesf = sb("onesf", [128, 128], F32)
    posi = sb("posi", [128, NT], I32)
    posi2 = sb("posi2", [128, 256], I32)
    posall = sb("posall", [128, S // 128], I32)
    ps = es.enter_context(nc.psum_tensor("ps", [128, 8, 512], F32))

    engsem = {e: es.enter_context(nc.semaphore("s_" + e)) for e in Prog.ENGS}
    dmasem = {}
    for e, n in Prog.NSLOT.items():
        for i in range(n):
            dmasem[(e, i)] = es.enter_context(nc.semaphore(f"d_{e}{i}"))
    block = es.enter_context(nc.Block())

    def dma(eng, out, in_, reads, writes):
        return P.add(eng, lambda e: e.dma_start(out=out, in_=in_), reads, writes, dma=True)

    def mm(out, lhsT, rhs, start, stop, reads, writes, tp=None):
        if tp is None:
            return P.add("pe", lambda e: e.matmul(out, lhsT, rhs, start=start, stop=stop), reads, writes)
        return P.add("pe", lambda e: e.matmul(out, lhsT, rhs, start=start, stop=stop, tile_position=tp),
                     reads, writes)

    t_ident = Tok("ident")
    t_ones = Tok("ones")

    def tr(out, in_, reads, writes):
        return P.add("pe", lambda e: e.transpose(out, in_, identb[:]), list(reads) + [t_ident], writes)

    def act(out, in_, func, reads, writes, scale=None, accum=None):
        kw = {}
        if scale is not None:
            kw["scale"] = scale
        if accum is not None:
            kw["accum_out"] = accum
        return P.add("act", lambda e: e.activation(out, in_, func, **kw), reads, writes)

    def dve(fn, reads, writes, eng="dve"):
        return P.add(eng, fn, reads, writes)

    def rstd_from_ss(out_col, ss_col, n, eps, rt, wt):
        dve(lambda e: e.tensor_scalar(out_col, ss_col, 1.0 / n, eps, ALU.mult, ALU.add), rt, wt)
        rsqrt_inplace(out_col, wt)

    def rsqrt_inplace(col, wt):
        act(col, col, AF.Sqrt, wt, wt)
        dve(lambda e: e.reciprocal(col, col), wt, wt)

    def psbf(b):
        return ps[:, b, :].bitcast(BF16)

    big_bf = big[:].rearrange("p t f -> p (t f)").bitcast(BF16)

    def bighi_f(t, a, b):
        return big[:, t, 1024 + a:1024 + b]

    def bighi_bf(t, a, b):
        base = t * 4096 + 2048
        return big_bf[:, base + a:base + b]

    t_ps = toks("ps", 8)
    t_xb = Tok("xb")
    t_hb = Tok("hb")
    t_gslot = Tok("gslot")

    def smc(i, n=1):
        return sm[:, i:i + n]

    dma("sp", identb[:], ident_d, [], [t_ident])
    P.add("pool", lambda e: e.memset(onesb[:], 1.0), [], [t_ones])
    P.add("pool", lambda e: e.memset(onesf[:], 1.0), [], [t_ones])
    g1c = sm[:, 256:272]
    g3c = sm[:, 272:288]
    subg = sm[:, 288:289]
    t_consts = Tok("consts")
    dma("sp", g1c, g1c_d, [], [t_consts])
    dma("sp", g3c, g3c_d, [], [t_consts])
    dma("sp", subg, subg_d, [], [t_consts])
    dma("sp", lamb[:], lam_d.partition_broadcast(128).rearrange("p o f -> p (o f)"), [], [t_consts])
    dma("sp", posall[:], posall_d, [], [t_consts])
    invf = sm[:, 300:332]
    dma("sp", invf, invf_d.partition_broadcast(128).rearrange("p o f -> p (o f)"), [], [t_consts])

    t_lam = Tok("lam")
    lt = sm[:, 340:404]
    dve(lambda e: e.tensor_tensor(lt, lamb[:, 0:64], lamb[:, 64:128], ALU.mult), [t_consts], [t_lam])
    dve(lambda e: e.reduce_sum(smc(290), lt, AX.X), [t_lam], [t_lam])
    dve(lambda e: e.tensor_tensor(lt, lamb[:, 128:192], lamb[:, 192:256], ALU.mult), [t_consts], [t_lam])
    dve(lambda e: e.reduce_sum(smc(291), lt, AX.X), [t_lam], [t_lam])
    act(sm[:, 292:294], sm[:, 290:292], AF.Exp, [t_lam], [t_lam])
    neglam = smc(294)
    dve(lambda e: e.tensor_tensor(neglam, smc(293), smc(292), ALU.subtract), [t_lam], [t_lam])
    dve(lambda e: e.tensor_scalar(neglam, neglam, -LAMBDA_INIT, None, ALU.add), [t_lam], [t_lam])
    gsc = smc(295)
    dve(lambda e: e.tensor_scalar(gsc, subg, 1.0 - LAMBDA_INIT, None, ALU.mult), [t_consts, t_lam], [t_lam])

    t_rope = Tok("rope")
    cosT = bighi_f(0, 0, 512).rearrange("p (t d) -> p t d", d=64)
    sinS = bighi_f(0, 512, 1024).rearrange("p (t d) -> p t d", d=64)
    posf = sm[:, 410:418]
    ang = sm[:, 0:256].rearrange("p (t d) -> p t d", d=32)
    tmpa = tmpa_t[:].rearrange("p (t d) -> p t d", d=32)
    SC = 1.0 - 2e-6
    C1 = 6.28125
    C2 = 2 * PI - C1
    kint = posi2[:].rearrange("p (t d) -> p t d", d=32)
    msk = lamb[:, 0:256].rearrange("p (t d) -> p t d", d=32)

    def wrap_pi(r):
        dve(lambda e: e.tensor_scalar(msk, r, -PI, 1e12, ALU.add, ALU.mult), [t_rope], [t_rope])
        dve(lambda e: e.tensor_scalar(msk, msk, 0.0, 1.0, ALU.max, ALU.min), [t_rope], [t_rope])
        dve(lambda e: e.scalar_tensor_tensor(r, msk, -2 * PI, r, ALU.mult, ALU.add), [t_rope], [t_rope])
        dve(lambda e: e.tensor_scalar(msk, r, PI, -1e12, ALU.add, ALU.mult), [t_rope], [t_rope])
        dve(lambda e: e.tensor_scalar(msk, msk, 0.0, 1.0, ALU.max, ALU.min), [t_rope], [t_rope])
        dve(lambda e: e.scalar_tensor_tensor(r, msk, 2 * PI, r, ALU.mult, ALU.add), [t_rope], [t_rope])

    def rope_tables(pos_i, cos_dst, sin_dst, wt):
        rw = [t_rope] + wt
        dve(lambda e: e.tensor_copy(posf, pos_i), [t_consts, t_lam], [t_rope])
        dve(lambda e: e.tensor_tensor(ang, posf.unsqueeze(2).broadcast_to([128, NT, 32]),
                                      invf.unsqueeze(1).broadcast_to([128, NT, 32]), ALU.mult),
            [t_consts], [t_rope])
        dve(lambda e: e.tensor_scalar(tmpa, ang, 1.0 / (2 * PI), None, ALU.mult), [t_rope], [t_rope])
        dve(lambda e: e.tensor_copy(kint, tmpa), [t_rope], [t_rope])
        dve(lambda e: e.tensor_copy(tmpa, kint), [t_rope], [t_rope])
        dve(lambda e: e.scalar_tensor_tensor(ang, tmpa, -C1, ang, ALU.mult, ALU.add), [t_rope], [t_rope])
        dve(lambda e: e.scalar_tensor_tensor(ang, tmpa, -C2, ang, ALU.mult, ALU.add), [t_rope], [t_rope])
        wrap_pi(ang)
        dve(lambda e: e.tensor_scalar(tmpa, ang, SC, None, ALU.mult), [t_rope], [t_rope])
        act(sin_dst[:, :, 32:64], tmpa, AF.Sin, [t_rope], rw)
        dve(lambda e: e.tensor_scalar(ang, ang, 0.5 * PI, None, ALU.add), [t_rope], [t_rope])
        wrap_pi(ang)
        dve(lambda e: e.tensor_scalar(tmpa, ang, SC, None, ALU.mult), [t_rope], [t_rope])
        act(cos_dst[:, :, 0:32], tmpa, AF.Sin, [t_rope], rw)
        dve(lambda e: e.tensor_scalar(sin_dst[:, :, 0:32], sin_dst[:, :, 32:64], -1.0, None, ALU.mult), rw, rw)
        dve(lambda e: e.tensor_copy(cos_dst[:, :, 32:64], cos_dst[:, :, 0:32]), rw, rw)

    t_ropeA = Tok("ropeA")
    cosA = [big[:, T // 16, (T % 16) * 64:(T % 16 + 1) * 64] for T in range(S // 128)]
    sinA = [big[:, 4 + T // 16, (T % 16) * 64:(T % 16 + 1) * 64] for T in range(S // 128)]
    def rope_tables_all(bch):
        T0 = bch * 8
        cdst = big[:, T0 // 16, (T0 % 16) * 64:(T0 % 16 + 8) * 64].rearrange("p (t d) -> p t d", d=64)
        sdst = big[:, 4 + T0 // 16, (T0 % 16) * 64:(T0 % 16 + 8) * 64].rearrange("p (t d) -> p t d", d=64)
        rope_tables(posall[:, T0:T0 + 8], cdst, sdst, [t_ropeA])

    rope_tables_all(0)

    t_gc = Tok("gmlpc")
    lnG = bighi_f(4, 0, 1024)
    lnB = bighi_f(5, 0, 1024)
    bsb = bighi_f(6, 0, 1024)
    wsT = bighi_bf(7, 0, 1024)
    dma("sp", lnG, lng_d.partition_broadcast(128).rearrange("p o f -> p (o f)"), [], [t_gc])
    dma("sp", lnB, lnb_d.partition_broadcast(128).rearrange("p o f -> p (o f)"), [], [t_gc])
    dma("sp", bsb, bs_d.partition_broadcast(128).rearrange("p o f -> p (o f)"), [], [t_gc])
    dma("pool", wsT, wsT_d, [], [t_gc])

    trc = [0]

    def to_feature_major(dst_fn, gcol, dst_toks, src=None, src_tok=None):
        src = hb if src is None else src
        src_tok = t_hb if src_tok is None else src_tok
        for half in range(2):
            b = 6 + (trc[0] % 2)
            trc[0] += 1
            pb = psbf(b)
            for j in range(8):
                k = half * 8 + j
                tr(pb[:, j * 128:(j + 1) * 128], src[:, k * 128:(k + 1) * 128], [src_tok], [t_ps[b]])
            for j in range(8):
                k = half * 8 + j
                o = dst_fn(k)
                i = pb[:, j * 128:(j + 1) * 128]
                sc = gcol[:, k:k + 1]
                P.add("dve", lambda e, o=o, i=i, sc=sc: e.tensor_scalar(o, i, sc, None, ALU.mult),
                      [t_ps[b], t_consts], dst_toks, relaxed=dst_toks)

    t_scr = toks("scr", 6)
    wbuf = [scr[:, 0:2, :].rearrange("p a (k f) -> p (a k) f", f=512),
            scr[:, 2:4, :].rearrange("p a (k f) -> p (a k) f", f=512)]
    t_wbuf = [[t_scr[0], t_scr[1]], [t_scr[2], t_scr[3]]]
    qT = scr[:, 4:6, :].rearrange("p a (h t) -> p (a h) t", t=1024)
    t_qT = [t_scr[4], t_scr[5]]
    win_v = win_d.rearrange("(k p) f -> p k f", p=128)

    def load_win(n, slot):
        dma("pool", wbuf[slot % 2], win_v[:, :, n * 512:(n + 1) * 512], [], t_wbuf[slot % 2])

    t_t1 = Tok("ropetmp")
    t1 = bighi_f(1, 0, 512).rearrange("p (m d) -> p m d", d=64)
    t2 = bighi_f(1, 512, 1024).rearrange("p (m d) -> p m d", d=64)
    qkb = [bighi_bf(2, 0, 512), bighi_bf(2, 512, 1024)]
    t_qkb = toks("qkb", 2)
    vln = bighi_bf(2, 1024, 2048)
    t_vln = Tok("vln")
    kst = [bighi_bf(3, 0, 512), bighi_bf(3, 512, 1024)]
    t_kst = toks("kst", 2)
    vst = [bighi_bf(3, 1024, 1536), bighi_bf(3, 1536, 2048)]
    t_vst = toks("vst", 2)
    t_lnst_l = toks("lnst", NT)


    bankc = [0]
    cnt_qk = [0]
    cnt_v = [0]

    def rope_only(b, cos_ap, sin_ap, cos_tok):
        x3 = ps[:, b, :].rearrange("p (m d) -> p m d", d=64)
        cb = cos_ap.unsqueeze(1).broadcast_to([128, 8, 64])
        s_lo = sin_ap[:, 0:32].unsqueeze(1).broadcast_to([128, 8, 32])
        s_hi = sin_ap[:, 32:64].unsqueeze(1).broadcast_to([128, 8, 32])
        i = cnt_qk[0] % 2
        cnt_qk[0] += 1
        dve(lambda e: e.tensor_tensor(t1, x3, cb, ALU.mult), [t_ps[b], cos_tok], [t_t1])
        dve(lambda e: e.tensor_tensor(t2[:, :, 0:32], x3[:, :, 32:64], s_lo, ALU.mult), [t_ps[b], cos_tok], [t_t1])
        dve(lambda e: e.tensor_tensor(t2[:, :, 32:64], x3[:, :, 0:32], s_hi, ALU.mult), [t_ps[b], cos_tok], [t_t1])
        dve(lambda e: e.tensor_tensor(qkb[i], bighi_f(1, 0, 512), bighi_f(1, 512, 1024), ALU.add),
            [t_t1], [t_qkb[i]])
        return i

    def tr4(i):
        pbi = 6 + (trc[0] % 2)
        trc[0] += 1
        pb = psbf(pbi)
        for j in range(4):
            tr(pb[:, j * 128:(j + 1) * 128], qkb[i][:, j * 128:(j + 1) * 128], [t_qkb[i]], [t_ps[pbi]])
        return pbi, pb[:, 0:512].rearrange("p (j t) -> p j t", t=128)

    def rope_to_bf16(b, cos_ap, sin_ap, cos_tok):
        i = rope_only(b, cos_ap, sin_ap, cos_tok)
        pbi, src = tr4(i)
        return i, pbi, src

    t_WKq = toks("WK", 4)
    t_WVq = toks("WV", 4)
    hTt = [scr[:, 4, 0:2048].rearrange("p (k t) -> p k t", t=128),
           scr[:, 4, 2048:4096].rearrange("p (k t) -> p k t", t=128)]
    t_hTt = [Tok("hTt0"), Tok("hTt1")]
    junk5 = scr[:, 5, 0:2048]
    kall_v = k_all.ap().rearrange("(h p) t -> p h t", p=128)
    t_kall = Tok("kall")
    t_vall = Tok("vall")
    t_ssK = Tok("ssK")
    NTA = S // 128

    xbufs = [xb, gslot]
    t_xbufs = [t_xb, t_gslot]

    def kv_load(T):
        dma("pool", xbufs[T % 2][:], xall_d[T * 128:(T + 1) * 128, :], [], [t_xbufs[T % 2]])

    t_ssKp = [Tok("ssK0"), Tok("ssK1")]

    def kv_norm(T):
        xs, tx = xbufs[T % 2], t_xbufs[T % 2]
        c0 = 500 + 2 * (T % 2)
        tss = t_ssKp[T % 2]
        act(junk5, xs[:], AF.Square, [tx], [tss], accum=smc(c0))
        rstd_from_ss(smc(c0 + 1), smc(c0), D, RMS_EPS, [tss], [tss])
        dve(lambda e: e.tensor_scalar(hb[:], xs[:], smc(c0 + 1), None, ALU.mult), [tx, tss], [t_hb])

    def kv_tr(T):
        i2 = T % 2
        for half in range(2):
            b = 6 + (trc[0] % 2)
            trc[0] += 1
            pb = psbf(b)
            for j in range(8):
                k = half * 8 + j
                tr(pb[:, j * 128:(j + 1) * 128], hb[:, k * 128:(k + 1) * 128], [t_hb], [t_ps[b]])
            dst = hTt[i2][:, half * 8:(half + 1) * 8, :]
            src = pb.rearrange("p (k t) -> p k t", t=128)
            if half == 0:
                act(dst, src, AF.Copy, [t_ps[b]], [t_hTt[i2]])
            else:
                dve(lambda e, dst=dst, src=src: e.tensor_copy(dst, src), [t_ps[b]], [t_hTt[i2]])
        if T < NT:
            hg = hTg[T % 2]
            thg = t_hTg[T % 2]
            for k in range(KC):
                P.add("dve", lambda e, k=k: e.tensor_scalar(hg[:, k, :], hTt[i2][:, k, :], g1c[:, k:k + 1], None,
                                                           ALU.mult),
                      [t_hTt[i2], t_consts], [thg], relaxed=[thg])
            dma("sp", hT_scr[:, :, T * 128:(T + 1) * 128], hg, [thg], [t_hTscr])

    hTg = [scr[:, 5, 2048:4096].rearrange("p (k t) -> p k t", t=128)] * 2
    t_hTg = [t_scr[5], t_scr[5]]
    t_hTscr = Tok("hTscr")
    kpend = {}

    def kv_mmK(T):
        i2 = T % 2
        kb = []
        for c in range(2):
            b = bankc[0] % 4
            bankc[0] += 1
            for k in range(KC):
                mm(ps[:, b, :], hTt[i2][:, k, :], hT[:, k, c * 512:(c + 1) * 512],
                   k == 0, k == KC - 1, [t_hTt[i2], t_WKq[k // 4]], [t_ps[b]])
            kb.append(b)
        kpend[T] = [rope_only(kb[c], cosA[T], sinA[T], t_ropeA) for c in range(2)]

    def kv_mmV(T):
        i2 = T % 2
        for c in range(2):
            b = bankc[0] % 4
            bankc[0] += 1
            for k in range(KC):
                mm(ps[:, b, :], hTt[i2][:, k, :], catT[:, k, c * 512:(c + 1) * 512],
                   k == 0, k == KC - 1, [t_hTt[i2], t_WVq[k // 4]], [t_ps[b]])
            i = cnt_v[0] % 2
            cnt_v[0] += 1
            act(vst[i], ps[:, b, :], AF.Copy, [t_ps[b]], [t_vst[i]])
            dma("sp", v_all.ap()[T * 128:(T + 1) * 128, c * 512:(c + 1) * 512], vst[i], [t_vst[i]], [t_vall])

    def kv_ktr(T):
        for c in range(2):
            i = kpend[T][c]
            pbi, src = tr4(i)
            kk = kst[i].rearrange("p (j t) -> p j t", t=128)
            act(kk, src, AF.Copy, [t_ps[pbi]], [t_kst[i]])
            dma("sp", kall_v[:, c * 4:(c + 1) * 4, T * 128:(T + 1) * 128], kk, [t_kst[i]], [t_kall])

    kv_load(0)
    kv_load(1)
    for c4 in range(4):
        dma("pool", hT[:, c4 * 4:(c4 + 1) * 4, :], win_v[:, c4 * 4:(c4 + 1) * 4, 1024:2048], [], [t_WKq[c4]])
    for c4 in range(4):
        dma("pool", catT[:, c4 * 4:(c4 + 1) * 4, :], win_v[:, c4 * 4:(c4 + 1) * 4, 2048:3072], [], [t_WVq[c4]])
    kv_norm(0)
    kv_tr(0)
    for k in range(KC):
        dve(lambda e, k=k: e.tensor_scalar(hT[:, k, :], hT[:, k, :], g1c[:, k:k + 1], None, ALU.mult),
            [t_WKq[k // 4], t_consts], [t_WKq[k // 4]])

    def fold_wv():
        for k in range(KC):
            if k % 2 == 0:
                act(catT[:, k, :], catT[:, k, :], AF.Copy, [t_WVq[k // 4], t_consts], [t_WVq[k // 4]],
                    scale=g1c[:, k:k + 1])
            else:
                dve(lambda e, k=k: e.tensor_scalar(catT[:, k, :], catT[:, k, :], g1c[:, k:k + 1], None, ALU.mult),
                    [t_WVq[k // 4], t_consts], [t_WVq[k // 4]])

    for T in range(NTA):
        if T % 8 == 0 and T + 8 < NTA:
            rope_tables_all(T // 8 + 1)
        if T + 1 < NTA:
            kv_norm(T + 1)
        if T + 2 < NTA:
            kv_load(T + 2)
        kv_mmK(T)
        if T == 0:
            fold_wv()
        if T + 1 < NTA:
            kv_tr(T + 1)
        kv_mmV(T)
        kv_ktr(T)

    t_scr[4] = Tok("scr4b", over=t_hTt)
    t_qT = [t_scr[4], t_scr[5]]
    t_gvg = [Tok(f"gvg{i}", over=[t_ropeA]) for i in range(NT)]
    t_cat = [[Tok(f"cat{j}_{c}", over=t_WVq) for c in range(2)] for j in range(KC)]

    t_hT = [Tok(f"hT{i}", over=t_WKq) for i in range(NT)]
    t_ssA_l = toks("ssA", NT)
    def a_load(t):
        dma("pool", xbufs[t % 2][:], x_d[t * 128:(t + 1) * 128, :], [], [t_xbufs[t % 2]])

    for q4 in range(4):
        dma("sp", hT[:, q4 * 4:(q4 + 1) * 4, :], hT_scr[:, q4 * 4:(q4 + 1) * 4, :], [t_hTscr], t_hT)

    dbg_items = []

    def finish():
        if dbg_d is not None:
            off = 0
            for (ap, n, rt) in dbg_items:
                dma("sp", dbg_d[:, off:off + n], ap, rt, [])
                off += n
        P.finalize()

        @block.tensor
        def _(e):
            P.emit("pe", e, engsem, dmasem)

        @block.scalar
        def _(e):
            P.emit("act", e, engsem, dmasem)

        @block.vector
        def _(e):
            P.emit("dve", e, engsem, dmasem)

        @block.gpsimd
        def _(e):
            P.emit("pool", e, engsem, dmasem)

        @block.sync
        def _(e):
            P.emit("sp", e, engsem, dmasem)

        es.close()
        return nc

    if stop_after == "A":
        if dbg_d is not None:
            dbg_items.append((hT[:, 0, :].bitcast(F32), 512, t_hT))
        return finish()

    ytmp = bighi_f(1, 0, 1024)

    vlnb = [vln, bighi_bf(3, 0, 1024)]
    t_vlnb = [t_vln, Tok("vln2", over=t_kst)]

    def gmlp_ln(t):
        gv = big[:, t, 0:1024]
        s1 = smc(440 + 2 * t, 2)
        mean = smc(460 + t)
        ssq = smc(470 + t)
        var = smc(480 + t)
        rs = smc(490 + t)
        act(hb[:, 0:1024], gv, AF.Square, [t_gvg[t]], [t_hb, t_lnst_l[t]], accum=ssq)
        dve(lambda e, mean=mean, s1=s1: e.reduce_sum(mean, s1, AX.X), [t_lnst_l[t]], [t_lnst_l[t]])
        dve(lambda e, mean=mean: e.tensor_scalar(mean, mean, 1.0 / 1024, None, ALU.mult), [t_lnst_l[t]], [t_lnst_l[t]])
        dve(lambda e, var=var, mean=mean: e.tensor_tensor(var, mean, mean, ALU.mult), [t_lnst_l[t]], [t_lnst_l[t]])
        dve(lambda e, var=var, ssq=ssq: e.scalar_tensor_tensor(var, ssq, 1.0 / 1024, var, ALU.mult, ALU.subtract),
            [t_lnst_l[t]], [t_lnst_l[t]])
        dve(lambda e, var=var, rs=rs: e.tensor_scalar(rs, var, LN_EPS, None, ALU.add), [t_lnst_l[t]], [t_lnst_l[t]])
        rsqrt_inplace(rs, [t_lnst_l[t]])
        dve(lambda e, gv=gv, mean=mean, rs=rs: e.tensor_scalar(gv, gv, mean, rs, ALU.subtract, ALU.mult),
            [t_lnst_l[t], t_gvg[t]], [t_gvg[t]])
        dve(lambda e, gv=gv: e.tensor_tensor(gv, gv, lnG, ALU.mult), [t_gvg[t], t_gc], [t_gvg[t]])
        vl, tvl = vlnb[t % 2], t_vlnb[t % 2]
        dve(lambda e, gv=gv, vl=vl: e.tensor_tensor(vl, gv, lnB, ALU.add), [t_gvg[t], t_gc], [tvl])

    def gmlp_mm(t):
        vl, tvl = vlnb[t % 2], t_vlnb[t % 2]
        for g in range(8):
            b = 4 + g // 4
            mm(ps[:, b, (g % 4) * 128:(g % 4 + 1) * 128], vl[:, g * 128:(g + 1) * 128],
               wsT[:, g * 128:(g + 1) * 128], True, True, [tvl, t_gc], [t_ps[b]])
        yv = ps[:, 4:6, :].rearrange("p a b -> p (a b)")
        dve(lambda e, yv=yv: e.tensor_tensor(ytmp, yv, bsb, ALU.add), [t_ps[4], t_ps[5], t_gc], [t_t1])
        tc = t // 4
        dve(lambda e, t=t: e.tensor_tensor(catT[:, 8:16, t * 128:(t + 1) * 128],
                                           ytmp.rearrange("p (g c) -> p g c", c=128),
                                           catT[:, 0:8, t * 128:(t + 1) * 128], ALU.mult),
            [t_t1] + [t_cat[j][tc] for j in range(8)], [t_cat[8 + j][tc] for j in range(8)])


    qprev = [None]

    def q_tr(i, n, t):
        pbi, src = tr4(i)
        dst = qT[:, n * 4:(n + 1) * 4, t * 128:(t + 1) * 128]
        act(dst, src, AF.Copy, [t_ps[pbi]], [t_qT[n]])

    chunks = [0, 1, 6, 7, 8, 9]
    load_win(chunks[0], 0)
    load_win(chunks[1], 1)
    for ci, n in enumerate(chunks):
        wb = wbuf[ci % 2]
        twb = t_wbuf[ci % 2]
        if n in (6, 7):
            for fc in range(4):
                j = (n - 6) * 4 + fc
                for tc in range(2):
                    b = bankc[0] % 4
                    bankc[0] += 1
                    for k in range(KC):
                        mm(ps[:, b, :], wb[:, k, fc * 128:(fc + 1) * 128], hT[:, k, tc * 512:(tc + 1) * 512],
                           k == 0, k == KC - 1, twb + t_hT[tc * 4:(tc + 1) * 4], [t_ps[b]])
                    act(catT[:, j, tc * 512:(tc + 1) * 512], ps[:, b, :], AF.Gelu, [t_ps[b]], [t_cat[j][tc]])
        else:
            for t in range(NT):
                b = bankc[0] % 4
                bankc[0] += 1
                for k in range(KC):
                    mm(ps[:, b, :], hT[:, k, t * 128:(t + 1) * 128], wb[:, k, :],
                       k == 0, k == KC - 1, twb + [t_hT[t]], [t_ps[b]])
                if n < 2:
                    i = rope_only(b, cosA[t], sinA[t], t_ropeA)
                    if qprev[0] is not None:
                        q_tr(*qprev[0])
                    qprev[0] = (i, n, t)
                else:
                    c = n - 8
                    act(big[:, t, c * 512:(c + 1) * 512], ps[:, b, :], AF.Gelu, [t_ps[b]],
                        [t_gvg[t], t_lnst_l[t]], accum=smc(440 + t * 2 + c))
                    if n == 9:
                        if t >= 2:
                            gmlp_mm(t - 2)
                        gmlp_ln(t)
            if n < 2 and qprev[0] is not None:
                q_tr(*qprev[0])
                qprev[0] = None
            if n == 9:
                gmlp_mm(NT - 2)
                gmlp_mm(NT - 1)
        if ci + 2 < len(chunks):
            load_win(chunks[ci + 2], ci)

    if stop_after == "B":
        if dbg_d is not None:
            dbg_items.append((hT[:, 0, :].bitcast(F32), 512, t_hT))
            dbg_items.append((qT[:, 0, :].bitcast(F32), 512, t_qT))
            dbg_items.append((catT[:, 8, :].bitcast(F32), 512, [t_cat[8][0], t_cat[8][1]]))
            dbg_items.append((catT[:, 0, :].bitcast(F32), 512, [t_cat[0][0], t_cat[0][1]]))
            dbg_items.append((sm[:, 0:512], 512, [t_lam, t_rope] + t_lnst_l + t_ssA_l))
        return finish()

    all_big = t_gvg + [t_rope, t_ropeA, t_gc, t_t1, t_vln] + t_qkb + t_kst + t_vst + t_vlnb
    t_KT = [Tok("KT0", over=t_hT), Tok("KT1", over=t_hT)]
    KTb = [hT[:, 0:8, :], hT[:, 8:16, :]]
    t_V = [Tok("V0", over=all_big), Tok("V1", over=all_big)]
    Vb = [big_bf[:, 4 * 4096:6 * 4096].rearrange("p (k d) -> p k d", d=128),
          big_bf[:, 6 * 4096:8 * 4096].rearrange("p (k d) -> p k d", d=128)]
    NE = 4
    Eb = [big_bf[:, i * 1024:(i + 1) * 1024] for i in range(NE)]
    t_E = [Tok(f"E{i}", over=all_big) for i in range(NE)]
    t_fin = Tok("fin", over=all_big)
    r1 = big[:, 2, 0:512]
    r2 = big[:, 2, 512:1024]
    o1 = big[:, 2, 1024:1536]
    o2 = big[:, 2, 1536:2048]
    rsn = big[:, 3, 0:512]
    sqb = big[:, 3, 512:768].bitcast(BF16)
    esum = big[:, 3, 1024:2048]
    t_esum = Tok("esum", over=all_big)
    t_esum1 = Tok("esum1", over=all_big)
    SB = [0, 2, 6]
    kvo = k_all.ap().rearrange("x (r t) -> x r t", r=NCORES)

    ec = [0]
    blocks = [(h, qc) for h in range(8) for qc in range(2)]

    def load_head(h):
        dma("sp", KTb[h % 2], kvo[h * 128:(h + 1) * 128, :, :], [t_kall], [t_KT[h % 2]])
        for r in range(NCORES):
            src = v_all.ap()[r * 1024:(r + 1) * 1024, h * 128:(h + 1) * 128]
            dma("sp", Vb[h % 2][:, r * 8:(r + 1) * 8, :], src.rearrange("(k p) d -> p k d", p=128),
                [t_vall], [t_V[h % 2]])

    def scores(h, qc, kt):
        KT, tk, tq = KTb[h % 2], t_KT[h % 2], t_qT[h // 4]
        qs = slice(qc * 512, (qc + 1) * 512)
        b0 = 2 * (kt % 2)
        ksl = (kt // 8, slice((kt % 8) * 128, (kt % 8 + 1) * 128))
        mm(ps[:, b0, :], KT[0:64, ksl[0], ksl[1]], qT[0:64, h, qs], True, True, [tk, tq], [t_ps[b0]])
        mm(ps[:, b0 + 1, :], KT[64:128, ksl[0], ksl[1]], qT[64:128, h, qs], True, True,
           [tk, tq], [t_ps[b0 + 1]])

    def stage_a0(h, qc):
        dve(lambda e: e.tensor_copy(o1, ps[:, 4, :]), [t_ps[4]], [t_fin])
        dve(lambda e: e.tensor_copy(o2, ps[:, 5, :]), [t_ps[5]], [t_fin])
        dve(lambda e: e.tensor_copy(esum[0:1, 0:512], ps[0:1, 6, :]), [t_ps[6]], [t_esum])
        dve(lambda e: e.tensor_copy(esum[32:33, 0:512], ps[32:33, 6, :]), [t_ps[6]], [t_esum])

    s_row = [esum[0:1, 0:512], esum[32:33, 0:512]]
    rr_row = [esum[0:1, 512:1024], esum[32:33, 512:1024]]
    hi_row = [big[0:1, 2, 0:256].bitcast(BF16), big[32:33, 2, 0:256].bitcast(BF16)]
    lo_row = [big[0:1, 2, 256:512].bitcast(BF16), big[32:33, 2, 256:512].bitcast(BF16)]
    tmp_row = [big[0:1, 2, 512:1024], big[32:33, 2, 512:1024]]
    one_row = [onesb[0:1, :], onesb[32:33, :]]
    t_rows = Tok("rows", over=all_big)

    def stage_a1(h, qc):
        for m in range(2):
            dve(lambda e, m=m: e.reciprocal(rr_row[m], s_row[m]), [t_esum], [t_rows])
            dve(lambda e, m=m: e.tensor_copy(hi_row[m], rr_row[m]), [t_rows], [t_rows])
            dve(lambda e, m=m: e.tensor_tensor(tmp_row[m], rr_row[m], hi_row[m], ALU.subtract), [t_rows], [t_rows])
            dve(lambda e, m=m: e.tensor_copy(lo_row[m], tmp_row[m]), [t_rows], [t_rows])

    def bcast_norm(m, o):
        mm(ps[:, 7, :], one_row[m], hi_row[m], True, False, [t_ones, t_rows], [t_ps[7]])
        mm(ps[:, 7, :], one_row[m], lo_row[m], False, True, [t_ones, t_rows], [t_ps[7]])
        dve(lambda e: e.tensor_tensor(o, o, ps[:, 7, :], ALU.mult), [t_ps[7], t_fin], [t_fin])

    def stage_a2(h, qc):
        bcast_norm(0, o1)

    def stage_a3(h, qc):
        bcast_norm(1, o2)
        dve(lambda e: e.scalar_tensor_tensor(o1, o2, neglam, o1, ALU.mult, ALU.add), [t_fin, t_lam], [t_fin])
        dve(lambda e: e.tensor_tensor(sqb, o1, o1, ALU.mult), [t_fin], [t_fin])

    def stage_b1(h, qc):
        mm(ps[:, 7, :], onesb[:], sqb, True, True, [t_ones, t_fin], [t_ps[7]])
        dve(lambda e: e.tensor_scalar(rsn, ps[:, 7, :], 1.0 / 128, SUBLN_EPS, ALU.mult, ALU.add),
            [t_ps[7]], [t_fin])

    def stage_b2(h, qc):
        qs = slice(qc * 512, (qc + 1) * 512)
        act(rsn, rsn, AF.Ln, [t_fin], [t_fin])
        act(rsn, rsn, AF.Exp, [t_fin], [t_fin], scale=-0.5)
        dve(lambda e: e.scalar_tensor_tensor(catT[:, h, qs], o1, gsc, rsn, ALU.mult, ALU.mult),
            [t_fin, t_lam], [t_cat[h][qc]])

    stages = {1: stage_a1, 6: stage_a2, 9: stage_a3, 13: stage_b1, 17: stage_b2}

    pending = [None]
    load_head(0)
    scores(0, 0, 0)
    for bi, (h, qc) in enumerate(blocks):
        if qc == 0 and h + 1 < 8:
            load_head(h + 1)
        V, tv = Vb[h % 2], t_V[h % 2]
        for kt in range(64):
            b0 = 2 * (kt % 2)
            if kt + 1 < 64:
                scores(h, qc, kt + 1)
            ei = ec[0] % NE
            ec[0] += 1
            E = Eb[ei]
            act(E, ps[:, b0:b0 + 2, :].rearrange("p a b -> p (a b)"), AF.Exp,
                [t_ps[b0], t_ps[b0 + 1]], [t_E[ei]], scale=0.125)
            st, sp_ = kt == 0, kt == 63
            mm(ps[0:32, 6, :], onesb[:, 0:32], E[:, 0:512], st, sp_, [t_ones, t_E[ei]], [t_ps[6]], tp=(0, 0))
            mm(ps[32:64, 6, :], onesb[:, 0:32], E[:, 512:1024], st, sp_, [t_ones, t_E[ei]], [t_ps[6]],
               tp=(0, 32))
            mm(ps[:, 4, :], V[:, kt, :], E[:, 0:512], st, sp_, [tv, t_E[ei]], [t_ps[4]])
            mm(ps[:, 5, :], V[:, kt, :], E[:, 512:1024], st, sp_, [tv, t_E[ei]], [t_ps[5]])
            if pending[0] is not None and kt in stages:
                stages[kt](*pending[0])
                if kt == 17:
                    pending[0] = None
        if bi + 1 < len(blocks):
            scores(blocks[bi + 1][0], blocks[bi + 1][1], 0)
        stage_a0(h, qc)
        pending[0] = (h, qc)
    for kt_ in sorted(stages):
        stages[kt_](*pending[0])

    if stop_after == "C":
        if dbg_d is not None:
            for j in range(8):
                dbg_items.append((catT[:, j, :].bitcast(F32), 512, [t_cat[j][0], t_cat[j][1]]))
        return finish()

    all_c = t_V + t_E + [t_fin]
    t_mix = [Tok(f"mix{t}", over=all_c) for t in range(NT)]
    t_wbufD = [[Tok("wD0a", over=t_scr[0:2]), Tok("wD0b")], [Tok("wD1a", over=t_scr[2:4]), Tok("wD1b")]]
    wout_v = wout_d.rearrange("(k p) f -> p k f", p=128)
    t_h2T = [Tok(f"h2T{t}", over=t_KT) for t in range(NT)]
    t_x1scr = toks("x1scr", NT)
    t_ssD_l = toks("ssD", NT)
    xbD = [xb[:], scr[:, 4, :].bitcast(F32)]
    t_xbD = [t_xb, Tok("xb2D", over=[t_scr[4]])]

    def load_wout(n):
        dma("pool", wbuf[n % 2], wout_v[:, :, n * 512:(n + 1) * 512], [], t_wbufD[n % 2])

    dma("sp", gslot[:], g2_d.partition_broadcast(128).rearrange("p o f -> p (o f)"), [], [t_gslot])
    load_wout(0)
    load_wout(1)
    for n in range(4):
        wb = wbuf[n % 2]
        for t in range(NT):
            b = bankc[0] % 4
            bankc[0] += 1
            tc = t // 4
            for k in range(KC):
                mm(ps[:, b, :], catT[:, k, t * 128:(t + 1) * 128], wb[:, k, :],
                   k == 0, k == KC - 1, t_wbufD[n % 2] + [t_cat[k][tc]], [t_ps[b]])
            act(big[:, t, n * 512:(n + 1) * 512], ps[:, b, :], AF.Copy, [t_ps[b]], [t_mix[t]])
        if n + 2 < 4:
            load_wout(n + 2)
    hbD = [hb, scr[:, 5, 2048:4096]]
    t_hbD = [t_hb, Tok("hb2D", over=[t_scr[5]])]

    def d_s1a(t):
        mx = big[:, t, :]
        act(junk5, mx, AF.Square, [t_mix[t]], [t_ssD_l[t]], accum=smc(96 + t))
        rstd_from_ss(smc(104 + t), smc(96 + t), D, RMS_EPS, [t_ssD_l[t]], [t_ssD_l[t]])
        xs, tx = xbD[t % 2], t_xbD[t % 2]
        dma("sp", xs, x_d[t * 128:(t + 1) * 128, :], [], [tx])
        dve(lambda e: e.scalar_tensor_tensor(mx, mx, smc(104 + t), gslot[:], ALU.mult, ALU.mult),
            [t_mix[t], t_ssD_l[t], t_gslot], [t_mix[t]])
        dve(lambda e: e.tensor_tensor(mx, mx, xs, ALU.add), [t_mix[t], tx], [t_mix[t]], eng="pool")
        dma("sp", x1_scr[t * 128:(t + 1) * 128, :], mx, [t_mix[t]], [t_x1scr[t]])
        act(junk5, mx, AF.Square, [t_mix[t]], [t_ssD_l[t]], accum=smc(112 + t))

    def d_s1b(t):
        mx = big[:, t, :]
        dve(lambda e: e.tensor_scalar(smc(120 + t), smc(112 + t), 1.0 / D, RMS_EPS, ALU.mult, ALU.add),
            [t_ssD_l[t]], [t_ssD_l[t]])
        act(smc(120 + t), smc(120 + t), AF.Sqrt, [t_ssD_l[t]], [t_ssD_l[t]])
        dve(lambda e: e.reciprocal(smc(120 + t), smc(120 + t)), [t_ssD_l[t]], [t_ssD_l[t]])
        dve(lambda e: e.tensor_scalar(hbD[t % 2][:], mx, smc(120 + t), None, ALU.mult),
            [t_mix[t], t_ssD_l[t]], [t_hbD[t % 2]])

    def d_s2(t):
        to_feature_major(lambda k, t=t: hT[:, k, t * 128:(t + 1) * 128], g3c, [t_h2T[t]],
                         src=hbD[t % 2], src_tok=t_hbD[t % 2])

    d_s1a(0)
    d_s1a(1)
    d_s1b(0)
    for t in range(NT):
        if t + 2 < NT:
            d_s1a(t + 2)
        d_s2(t)
        if t + 1 < NT:
            d_s1b(t + 1)

    if stop_after == "D":
        for t in range(NT):
            dma("sp", out_d[t * 128:(t + 1) * 128, :], big[:, t, :], [t_mix[t]], [])
        return finish()

    NG = DFF // 512
    t_f = [Tok(f"f{t}", over=[t_mix[t]]) for t in range(NT)]
    gub = [scr[:, i, :].rearrange("p (k f) -> p k f", f=256) for i in range(3)]
    gub3 = [hb[:].rearrange("p (k f) -> p k f", f=256),
            xb[:, 1024:2048].bitcast(BF16).rearrange("p (k f) -> p k f", f=256)]
    NGB = 4
    t_gub = [Tok("gub0", over=t_wbufD[0]), Tok("gub1", over=t_wbufD[0]), Tok("gub2", over=t_wbufD[1]),
             Tok("gub3", over=[t_hb, t_xb] + t_hbD)]

    def gub_k(i, k, c0, c1):
        if i < 3:
            return gub[i][:, k, c0:c1]
        return gub3[k // 8][:, k % 8, c0:c1]
    actT = [scr[:, 3, :].rearrange("p (c t) -> p c t", t=1024), scr[:, 4, :].rearrange("p (c t) -> p c t", t=1024)]
    t_actT = [Tok("actT0", over=t_wbufD[1]), Tok("actT1", over=[t_scr[4]])]
    all_cat = [t_cat[j][c] for j in range(KC) for c in range(2)]
    wdb = [catT[:, 0:8, :].rearrange("p a t -> p (a t)").rearrange("p (c n) -> p c n", n=2048),
           catT[:, 8:16, :].rearrange("p a t -> p (a t)").rearrange("p (c n) -> p c n", n=2048)]
    t_wdb = [Tok("wdb0", over=all_cat), Tok("wdb1", over=all_cat)]
    sgt = [xb[:, 0:512], xb[:, 512:1024]]
    t_sgt = [Tok("sgt0", over=[t_xb]), Tok("sgt1", over=[t_xb])]
    wg_v = wg_d.rearrange("(k p) f -> p k f", p=128)
    wu_v = wu_d.rearrange("(k p) f -> p k f", p=128)
    wd_v = wd_d.rearrange("(c p) n -> p c n", p=128)

    pieces = [(g, c, w) for g in range(NG) for c in range(2) for w in range(2)]

    def load_piece(i):
        g, c, w = pieces[i]
        src = (wg_v if w == 0 else wu_v)[:, :, g * 512 + c * 256:g * 512 + (c + 1) * 256]
        bi_ = i % NGB
        if bi_ < 3:
            dma("pool", gub[bi_], src, [], [t_gub[bi_]])
        else:
            dma("pool", gub3[0], src[:, 0:8, :], [], [t_gub[3]])
            dma("pool", gub3[1], src[:, 8:16, :], [], [t_gub[3]])

    def load_wd(g):
        dma("pool", wdb[g % 2], wd_v[:, g * 4:(g + 1) * 4, :], [], [t_wdb[g % 2]])

    dma("sp", gslot[:], g4_d.partition_broadcast(128).rearrange("p o f -> p (o f)"), [], [t_gslot])
    load_piece(0)
    load_piece(1)
    load_wd(0)
    sgc = [0]
    dbank = [0]
    def ffn_gu(g):
        at = actT[g % 2]
        tat = t_actT[g % 2]
        for c in range(2):
            ig = (g * 2 + c) * 2
            if ig + 2 < len(pieces):
                load_piece(ig + 2)
                load_piece(ig + 3)
            ig_g, tg = ig % NGB, t_gub[ig % NGB]
            ig_u, tu = (ig + 1) % NGB, t_gub[(ig + 1) % NGB]
            for fcl in range(2):
                fl = c * 2 + fcl
                for tc in range(2):
                    bg = tc
                    bu = 2 + tc
                    for k in range(KC):
                        mm(ps[:, bg, :], gub_k(ig_g, k, fcl * 128, (fcl + 1) * 128), hT[:, k, tc * 512:(tc + 1) * 512],
                           k == 0, k == KC - 1, [tg] + t_h2T[tc * 4:(tc + 1) * 4], [t_ps[bg]])
                    for k in range(KC):
                        mm(ps[:, bu, :], gub_k(ig_u, k, fcl * 128, (fcl + 1) * 128), hT[:, k, tc * 512:(tc + 1) * 512],
                           k == 0, k == KC - 1, [tu] + t_h2T[tc * 4:(tc + 1) * 4], [t_ps[bu]])
                    si = sgc[0] % 2
                    sgc[0] += 1
                    act(sgt[si], ps[:, bg, :], AF.Silu, [t_ps[bg]], [t_sgt[si]])
                    dve(lambda e, at=at, fl=fl, tc=tc, si=si, bu=bu:
                        e.tensor_tensor(at[:, fl, tc * 512:(tc + 1) * 512], ps[:, bu, :], sgt[si], ALU.mult),
                        [t_ps[bu], t_sgt[si]], [tat])

    def ffn_down(g):
        at = actT[g % 2]
        tat = t_actT[g % 2]
        if g + 1 < NG:
            load_wd(g + 1)
        wd = wdb[g % 2]
        twd = t_wdb[g % 2]
        if g == NG - 1:
            xbE = [xb[:], scr[:, 0, :].bitcast(F32)]
            t_xbE = [Tok("xbE0", over=t_sgt + [t_gub[3]]), Tok("xbE1", over=t_gub)]
            t_ssE_l = toks("ssE", NT)

            def epilogue(t):
                fv = big[:, t, :]
                xs, tx = xbE[t % 2], t_xbE[t % 2]
                act(junk5, fv, AF.Square, [t_f[t]], [t_ssE_l[t]], accum=smc(128 + t))
                rstd_from_ss(smc(136 + t), smc(128 + t), D, RMS_EPS, [t_ssE_l[t]], [t_ssE_l[t]])
                dma("sp", xs, x1_scr[t * 128:(t + 1) * 128, :], [t_x1scr[t]], [tx])
                dve(lambda e: e.scalar_tensor_tensor(fv, fv, smc(136 + t), gslot[:], ALU.mult, ALU.mult),
                    [t_f[t], t_ssE_l[t], t_gslot], [t_f[t]])
                dve(lambda e: e.tensor_tensor(fv, fv, xs, ALU.add), [t_f[t], tx], [t_f[t]], eng="pool")
                dma("sp", out_d[t * 128:(t + 1) * 128, :], fv, [t_f[t]], [])
        for t in range(NT):
            for n in range(4):
                b = 4 + dbank[0] % 4
                dbank[0] += 1
                for fl in range(4):
                    mm(ps[:, b, :], at[:, fl, t * 128:(t + 1) * 128], wd[:, fl, n * 512:(n + 1) * 512],
                       fl == 0, fl == 3, [tat, twd], [t_ps[b]])
                fv = big[:, t, n * 512:(n + 1) * 512]
                if g == 0:
                    dve(lambda e, fv=fv, b=b: e.tensor_copy(fv, ps[:, b, :]), [t_ps[b]], [t_f[t]])
                else:
                    dve(lambda e, fv=fv, b=b: e.tensor_tensor(fv, ps[:, b, :], fv, ALU.add), [t_ps[b], t_f[t]], [t_f[t]])
            if g == NG - 1:
                epilogue(t)

    ffn_gu(0)
    for g in range(NG):
        if g + 1 < NG:
            ffn_gu(g + 1)
        ffn_down(g)

    return finish()


def make_in_maps(x, positions, pre_mix_g, w_in, lambda_q1, lambda_k1, lambda_q2, lambda_k2,
                 subln_g, gmlp_ln_g, gmlp_ln_b, w_s, b_s, w_out, post_mix_g,
                 pre_ffn_g, w_gate, w_up, w_down, post_ffn_g):
    f = lambda a: np.ascontiguousarray(np.asarray(a, dtype=np.float32))
    x = f(x)[0]
    pos = np.asarray(positions, dtype=np.int32)[0]
    invf = (10000.0 ** (-(np.arange(0, 64, 2, dtype=np.float32) / np.float32(64)))).astype(np.float32)[None, :]
    ident = np.eye(128, dtype=np.float32).astype(ml_dtypes.bfloat16)
    common = {
        "invf": invf,
        "ident": ident,
        "g1c": np.ascontiguousarray(f(pre_mix_g)[0].reshape(KC, 128).T),
        "g3c": np.ascontiguousarray(f(pre_ffn_g)[0].reshape(KC, 128).T),
        "g2": f(post_mix_g)[0][None, :],
        "g4": f(post_ffn_g)[0][None, :],
        "lamv": np.concatenate([f(lambda_q1)[0], f(lambda_k1)[0], f(lambda_q2)[0], f(lambda_k2)[0]])[None, :],
        "subg": f(subln_g)[0][:, None],
        "lng": f(gmlp_ln_g)[0][None, :],
        "lnb": f(gmlp_ln_b)[0][None, :],
        "wsT": np.ascontiguousarray(f(w_s)[0].transpose(2, 0, 1).reshape(128, 1024)),
        "bs": f(b_s)[0].reshape(1, 1024),
        "w_in": f(w_in)[0],
        "w_out": f(w_out)[0],
        "w_gate": f(w_gate)[0],
        "w_up": f(w_up)[0],
        "w_down": f(w_down)[0],
    }
    in_maps = []
    for c in range(NCORES):
        m = dict(common)
        m["x"] = np.ascontiguousarray(x[c * TOK:(c + 1) * TOK])
        m["x_all"] = np.ascontiguousarray(np.roll(x, -c * TOK, axis=0))
        m["pos_all"] = np.ascontiguousarray(np.roll(pos, -c * TOK).reshape(S // 128, 128).T)
        m["pos"] = np.ascontiguousarray(pos[c * TOK:(c + 1) * TOK].reshape(NT, 128).T)
        in_maps.append(m)
    return in_maps


def kernel(**inputs):
    in_maps = make_in_maps(**inputs)
    nc = build_nc()
    res = run_bass_kernel_spmd(nc, in_maps, core_ids=list(range(NCORES)))
    out = np.concatenate([np.asarray(r["out"], dtype=np.float32) for r in res.results], axis=0)
    return out[None, :, :]
```
